# Optimizing a Trainium2 kernel written in Bass

```python
import math
import jax, jax.numpy as jnp
from jax import lax
import numpy as np

D_MODEL = 1024
BATCH = 2
SEQ = 8192
DEPTH = 1

N_META = 16
D_MIX = 2 * D_MODEL
D_SSD = D_MIX // 2
SSD_HEAD_DIM = 64
SSD_HEADS = D_SSD // SSD_HEAD_DIM
SSD_GROUPS = 2
SSD_HEADS_PER_GROUP = SSD_HEADS // SSD_GROUPS
SSD_STATE = 128
SSD_CONV = 4
SSD_CHUNK = 256
D_XBC = D_SSD + 2 * SSD_GROUPS * SSD_STATE
D_S5 = D_MIX - D_SSD
S5_GROUP_WIDTH = 16
S5_GROUPS = D_S5 // S5_GROUP_WIDTH
S5_STATE = 64
D_FF = 4 * D_MODEL
D_IN_PROJ = D_SSD + D_XBC + SSD_HEADS + D_S5
NORM_EPS = 1e-5
DT_MIN = 0.001
DT_MAX = 0.1

kernel_name = 'hymba_ssd_s5_hybrid_block'


def _rmsnorm(x, g):
    xf = x.astype(jnp.float32)
    y = xf * lax.rsqrt(jnp.mean(xf * xf, axis=-1, keepdims=True) + NORM_EPS)
    return (y * g.astype(jnp.float32)).astype(x.dtype)


def _causal_depthwise_conv(x, w, b):
    k, c = w.shape
    y = lax.conv_general_dilated(x, w[:, None, :].astype(x.dtype), window_strides=(1,), padding=[(k - 1, 0)], dimension_numbers=('NWC', 'WIO', 'NWC'), feature_group_count=c)
    return y + b.astype(x.dtype)


def _ssd_mixer(xbc, dt_raw, z, dt_bias, a_log, d_skip, g_norm):
    bsz, length, _ = xbc.shape
    f32 = jnp.float32
    xbc = xbc.astype(f32)
    x_in = xbc[..., :D_SSD]
    b_in = xbc[..., D_SSD:D_SSD + SSD_GROUPS * SSD_STATE]
    c_in = xbc[..., D_SSD + SSD_GROUPS * SSD_STATE:]
    dt = jax.nn.softplus(dt_raw.astype(f32) + dt_bias.astype(f32))
    front = SSD_CHUNK - N_META
    n_real_chunks = -(-(length - N_META) // SSD_CHUNK)
    total = SSD_CHUNK * (1 + n_real_chunks)
    back = total - front - length
    pad = lambda t: jnp.pad(t, ((0, 0), (front, back), (0, 0)))
    n_chunks = total // SSD_CHUNK
    shp = (bsz, n_chunks, SSD_CHUNK, SSD_GROUPS)
    xc = pad(x_in).reshape(shp + (SSD_HEADS_PER_GROUP, SSD_HEAD_DIM))
    bc = pad(b_in).reshape(shp + (SSD_STATE,))
    cc = pad(c_in).reshape(shp + (SSD_STATE,))
    dtc = pad(dt).reshape(shp + (SSD_HEADS_PER_GROUP,))
    a = -jnp.exp(a_log.astype(f32)).reshape(SSD_GROUPS, SSD_HEADS_PER_GROUP)
    a_cs = jnp.cumsum(dtc * a, axis=2)
    xdt = xc * dtc[..., None]
    causal = jnp.tril(jnp.ones((SSD_CHUNK, SSD_CHUNK), dtype=bool))[:, :, None, None]
    seg = a_cs[:, :, :, None] - a_cs[:, :, None, :]
    decay = jnp.exp(jnp.where(causal, seg, -jnp.inf))
    cb = jnp.einsum('bclgn,bcsgn->bclsg', cc, bc)
    y_diag = jnp.einsum('bclsg,bclsgr,bcsgrp->bclgrp', cb, decay, xdt)
    decay_to_end = jnp.exp(a_cs[:, :, -1:] - a_cs)
    chunk_states = jnp.einsum('bclgn,bclgr,bclgrp->bcgrpn', bc, decay_to_end, xdt)
    chunk_decay = jnp.exp(a_cs[:, :, -1])

    def step(state, inp):
        dec, st = inp
        return state * dec[..., None, None] + st, state

    init = jnp.zeros((bsz, SSD_GROUPS, SSD_HEADS_PER_GROUP, SSD_HEAD_DIM, SSD_STATE), f32)
    _, prev = lax.scan(step, init, (jnp.moveaxis(chunk_decay, 1, 0), jnp.moveaxis(chunk_states, 1, 0)))
    prev = jnp.moveaxis(prev, 0, 1)
    y_off = jnp.einsum('bclgn,bcgrpn,bclgr->bclgrp', cc, prev, jnp.exp(a_cs))
    d = d_skip.astype(f32).reshape(SSD_GROUPS, SSD_HEADS_PER_GROUP, 1)
    y = (y_diag + y_off + xc * d).reshape(bsz, total, D_SSD)[:, front:front + length]
    y = y * jax.nn.silu(z.astype(f32))
    return _rmsnorm(y, g_norm)


def _s5_mixer(u, lam_re, lam_im, log_step, b_re, b_im, c_re, c_im, d_skip, w_glu, b_glu, g_norm):
    bsz, length, _ = u.shape
    f32 = jnp.float32
    u = u.astype(f32).reshape(bsz, length, S5_GROUPS, S5_GROUP_WIDTH)
    lr = lam_re.astype(f32)
    li = lam_im.astype(f32)
    step = jnp.exp(log_step.astype(f32))[:, None]
    mag = jnp.exp(lr * step)
    ab_re = mag * jnp.cos(li * step)
    ab_im = mag * jnp.sin(li * step)
    den = lr * lr + li * li
    coef_re = ((ab_re - 1.0) * lr + ab_im * li) / den
    coef_im = (ab_im * lr - (ab_re - 1.0) * li) / den
    br = b_re.astype(f32)
    bi = b_im.astype(f32)
    bb_re = coef_re[..., None] * br - coef_im[..., None] * bi
    bb_im = coef_re[..., None] * bi + coef_im[..., None] * br
    bu_re = jnp.einsum('blgh,gph->blgp', u, bb_re)
    bu_im = jnp.einsum('blgh,gph->blgp', u, bb_im)
    a_re = jnp.broadcast_to(ab_re, (1, length) + ab_re.shape)
    a_im = jnp.broadcast_to(ab_im, (1, length) + ab_im.shape)

    def combine(e_i, e_j):
        ar_i, ai_i, br_i, bi_i = e_i
        ar_j, ai_j, br_j, bi_j = e_j
        return (ar_j * ar_i - ai_j * ai_i,
                ar_j * ai_i + ai_j * ar_i,
                ar_j * br_i - ai_j * bi_i + br_j,
                ar_j * bi_i + ai_j * br_i + bi_j)

    _, _, s_re, s_im = lax.associative_scan(combine, (a_re, a_im, bu_re, bu_im), axis=1)
    y = (jnp.einsum('blgp,ghp->blgh', s_re, c_re.astype(f32))
         - jnp.einsum('blgp,ghp->blgh', s_im, c_im.astype(f32))
         + u * d_skip.astype(f32))
    y = jax.nn.gelu(y.reshape(bsz, length, D_S5), approximate=False)
    v = y @ w_glu.astype(f32) + b_glu.astype(f32)
    y = v[..., :D_S5] * jax.nn.sigmoid(v[..., D_S5:])
    return _rmsnorm(y, g_norm)


def setup_inputs(seed: int = 0) -> dict:
    key = jax.random.key(seed)
    ks = jax.random.split(key, 32)
    f32 = jnp.float32
    nrm = lambda k, s, sc: jax.random.normal(k, s, f32) * sc
    gain = lambda k, s: 1.0 + 0.01 * jax.random.normal(k, s, f32)
    dt = jnp.exp(jax.random.uniform(ks[6], (DEPTH, SSD_HEADS), f32) * (math.log(DT_MAX) - math.log(DT_MIN)) + math.log(DT_MIN))
    dt = jnp.maximum(dt, 1e-4)
    dt_bias = dt + jnp.log(-jnp.expm1(-dt))
    n_idx = jnp.arange(S5_STATE, dtype=f32)
    return {
        'x': nrm(ks[0], (BATCH, SEQ, D_MODEL), 1.0),
        'meta_tokens': nrm(ks[1], (N_META, D_MODEL), 1.0),
        'g_mix': gain(ks[2], (DEPTH, D_MODEL)),
        'w_in': nrm(ks[3], (DEPTH, D_MODEL, D_IN_PROJ), D_MODEL ** -0.5),
        'conv_w': nrm(ks[4], (DEPTH, SSD_CONV, D_XBC), SSD_CONV ** -0.5),
        'conv_b': nrm(ks[5], (DEPTH, D_XBC), 0.01),
        'dt_bias': dt_bias,
        'a_log': jnp.log(jax.random.uniform(ks[7], (DEPTH, SSD_HEADS), f32, 1.0, 16.0)),
        'd_ssd': gain(ks[8], (DEPTH, SSD_HEADS)),
        'g_ssd': gain(ks[9], (DEPTH, D_SSD)),
        'lam_re': -0.5 + nrm(ks[10], (DEPTH, S5_GROUPS, S5_STATE), 0.01),
        'lam_im': math.pi * n_idx + nrm(ks[11], (DEPTH, S5_GROUPS, S5_STATE), 0.01),
        'log_step': jax.random.uniform(ks[12], (DEPTH, S5_GROUPS), f32, math.log(DT_MIN), math.log(DT_MAX)),
        'b_re': nrm(ks[13], (DEPTH, S5_GROUPS, S5_STATE, S5_GROUP_WIDTH), (2 * S5_GROUP_WIDTH) ** -0.5),
        'b_im': nrm(ks[14], (DEPTH, S5_GROUPS, S5_STATE, S5_GROUP_WIDTH), (2 * S5_GROUP_WIDTH) ** -0.5),
        'c_re': nrm(ks[15], (DEPTH, S5_GROUPS, S5_GROUP_WIDTH, S5_STATE), S5_STATE ** -0.5),
        'c_im': nrm(ks[16], (DEPTH, S5_GROUPS, S5_GROUP_WIDTH, S5_STATE), S5_STATE ** -0.5),
        'd_s5': nrm(ks[17], (DEPTH, S5_GROUPS, S5_GROUP_WIDTH), 1.0),
        'w_glu': nrm(ks[18], (DEPTH, D_S5, 2 * D_S5), D_S5 ** -0.5),
        'b_glu': nrm(ks[19], (DEPTH, 2 * D_S5), 0.01),
        'g_s5': gain(ks[20], (DEPTH, D_S5)),
        'w_out': nrm(ks[21], (DEPTH, D_MIX, D_MODEL), D_MIX ** -0.5),
        'g_mlp': gain(ks[22], (DEPTH, D_MODEL)),
        'w_up': nrm(ks[23], (DEPTH, D_MODEL, D_FF), D_MODEL ** -0.5),
        'w_down': nrm(ks[24], (DEPTH, D_FF, D_MODEL), D_FF ** -0.5),
        'g_final': gain(ks[25], (D_MODEL,)),
    }


def reference(x, meta_tokens, g_mix, w_in, conv_w, conv_b, dt_bias, a_log, d_ssd, g_ssd, lam_re, lam_im, log_step, b_re, b_im, c_re, c_im, d_s5, w_glu, b_glu, g_s5, w_out, g_mlp, w_up, w_down, g_final):
    bsz = x.shape[0]
    meta = jnp.broadcast_to(meta_tokens.astype(x.dtype)[None], (bsz, N_META, D_MODEL))
    h = jnp.concatenate([meta, x], axis=1)
    o_xbc = D_SSD
    o_dt = D_SSD + D_XBC
    o_u = o_dt + SSD_HEADS
    for layer in range(DEPTH):
        n = _rmsnorm(h, g_mix[layer])
        proj = n @ w_in[layer]
        z = proj[..., :o_xbc]
        xbc = jax.nn.silu(_causal_depthwise_conv(proj[..., o_xbc:o_dt], conv_w[layer], conv_b[layer]))
        dt_raw = proj[..., o_dt:o_u]
        u = proj[..., o_u:]
        y_ssd = _ssd_mixer(xbc, dt_raw, z, dt_bias[layer], a_log[layer], d_ssd[layer], g_ssd[layer])
        y_s5 = _s5_mixer(u, lam_re[layer], lam_im[layer], log_step[layer], b_re[layer], b_im[layer], c_re[layer], c_im[layer], d_s5[layer], w_glu[layer], b_glu[layer], g_s5[layer])
        mix = jnp.concatenate([y_ssd.astype(h.dtype), y_s5.astype(h.dtype)], axis=-1)
        h = h + mix @ w_out[layer]
        m = _rmsnorm(h, g_mlp[layer]) @ w_up[layer]
        h = h + jnp.square(jax.nn.relu(m)) @ w_down[layer]
    return _rmsnorm(h, g_final)[:, N_META:].astype(x.dtype)
```

```python
import numpy as np
from contextlib import ExitStack
import concourse.bass as bass
import concourse.mybir as mybir
from concourse.bass_utils import run_bass_kernel_spmd

F32 = mybir.dt.float32
BF16 = mybir.dt.bfloat16
AF = mybir.ActivationFunctionType
ALU = mybir.AluOpType

D = 1024
SEQ = 8192
NMETA = 16
NCH = SEQ // 128
MAGIC = 12582912.0
TWO_PI = 6.283185


class Res:
    def __init__(self, name):
        self.name = name
        self.lw = None
        self.rd = []
        self.dsem = None
        self.dcnt = 0


class Prog:
    ENG = ('pe', 'act', 'dve', 'pool', 'sp')

    def __init__(self, nc, es):
        self.nc = nc
        self.es = es
        self.q = {e: [] for e in self.ENG}
        self.cnt = {e: 0 for e in self.ENG}
        self.sem = {e: es.enter_context(nc.semaphore('s_' + e)) for e in self.ENG}
        self.waited = {e: {} for e in self.ENG}
        self.nd = 0

    def _need(self, eng, tok, waits):
        if tok is None:
            return
        if tok[0] == 'e':
            _, e2, idx = tok
            if e2 == eng and eng == 'pe':
                return
            key = ('e', e2)
            val = idx
            sem = self.sem[e2]
        else:
            r = tok[1]
            key = ('d', id(r))
            val = r.dcnt
            sem = r.dsem
        if self.waited[eng].get(key, 0) >= val:
            return
        self.waited[eng][key] = val
        waits.append((sem, val))

    def op(self, eng, fn, reads=(), writes=(), dma=False):
        waits = []
        for r in reads:
            self._need(eng, r.lw, waits)
        for w in writes:
            self._need(eng, w.lw, waits)
            for t in w.rd:
                self._need(eng, t, waits)
        if dma:
            w = writes[0]
            if w.dsem is None:
                w.dsem = self.es.enter_context(self.nc.semaphore('d_%d' % self.nd))
                self.nd += 1
            w.dcnt += 16
            tok = ('d', w)
            inc = (w.dsem, 16)
        else:
            self.cnt[eng] += 1
            tok = ('e', eng, self.cnt[eng])
            inc = (self.sem[eng], 1)
        for r in reads:
            r.rd.append(tok)
            if len(r.rd) > 64:
                r.rd = r.rd[-64:]
        for w in writes:
            w.lw = tok
            w.rd = []
        self.q[eng].append((waits, fn, inc))

    def wait_all(self, eng, ress):
        waits = []
        for r in ress:
            self._need(eng, r.lw, waits)
        self.q[eng].append((waits, None, None))

    def emit(self):
        nc = self.nc
        with nc.Block() as block:
            def run(e, handle):
                for waits, fn, inc in self.q[e]:
                    for sem, val in waits:
                        handle.wait_ge(sem, val)
                    if fn is not None:
                        fn(handle).then_inc(inc[0], inc[1])

            @block.tensor
            def _(h):
                run('pe', h)

            @block.scalar
            def _(h):
                run('act', h)

            @block.vector
            def _(h):
                run('dve', h)

            @block.gpsimd
            def _(h):
                run('pool', h)

            @block.sync
            def _(h):
                run('sp', h)


NB_IN, NB_GLU, NB_OUT, NB_UP, NB_DN = 8, 4, 4, 8, 8
OFF_IN = 0
OFF_GLU = OFF_IN + NB_IN
OFF_OUT = OFF_GLU + NB_GLU
OFF_UP = OFF_OUT + NB_OUT
OFF_DN = OFF_UP + NB_UP
NBLK = OFF_DN + NB_DN


class _Stop(Exception):
    pass


def build(nchunks=NCH, stop=None, stop_chunk=0):
    nc = bass.Bass("TRN2", target_bir_lowering=False)
    dram = lambda n, s, dt=F32, kind="ExternalInput": nc.dram_tensor(n, s, dt, kind=kind).ap()
    NT = 128 + 128 * nchunks
    x_d = dram("x", [NT, D])
    w_in_d = dram("w_in", [D, 4096])
    w_glu_d = dram("w_glu", [D, 2048])
    w_out_d = dram("w_out", [2048, D])
    w_up_d = dram("w_up", [D, 4096])
    w_dn_d = dram("w_down", [4096, D])
    vecs_d = dram("vecs", [128, 1024 + 48])
    bglu_d = dram("bglu", [128, 2048])
    gfm_d = dram("gfm", [128, 40])
    convp_d = dram("convp", [128, 12 * 5])
    consts_d = dram("consts", [128, 5 * 128])
    s5p_d = dram("s5p", [128, 32 * 3])
    s5b_d = dram("s5b", [128, 32 * 2 * 16])
    s5c_d = dram("s5c", [128, 32 * 2 * 16])
    s5d_d = dram("s5d", [128, 8])
    out_d = dram("out", [128 * nchunks, D], kind="ExternalOutput")
    wscr = dram("wscr", [NBLK, 128, 8 * 512], BF16, kind="Internal")
    dbg_d = dram("dbg", [128, 8192], kind="ExternalOutput") if stop is not None else None

    with ExitStack() as es:
        P = Prog(nc, es)
        cnt = [0]

        def sb(shape, dt=F32, name=None):
            cnt[0] += 1
            nm = name or ("t%d" % cnt[0])
            return es.enter_context(nc.sbuf_tensor(nm, shape, dt)), Res(nm)

        def psum(shape, dt=F32):
            cnt[0] += 1
            nm = "ps%d" % cnt[0]
            return es.enter_context(nc.psum_tensor(nm, shape, dt)), Res(nm)

        V = lambda fn, r=(), w=(): P.op('dve', fn, r, w)
        A = lambda fn, r=(), w=(): P.op('act', fn, r, w)
        G = lambda fn, r=(), w=(): P.op('pool', fn, r, w)
        T = lambda fn, r=(), w=(): P.op('pe', fn, r, w)
        LD = lambda fn, r=(), w=(): P.op('sp', fn, r, w, dma=True)
        GD = lambda fn, r=(), w=(): P.op('pool', fn, r, w, dma=True)

        r_dbg = Res("dbg")

        def stage(name, dumps, ci=None):
            if stop != name or (ci is not None and ci != stop_chunk):
                return
            col = 0
            for ap, r, n in dumps:
                GD(lambda e, ap=ap, col=col, n=n: e.dma_start(out=dbg_d[:, col:col + n], in_=ap), [r], [r_dbg])
                col += n
            raise _Stop()

        pT, r_pT = psum([128, 1024], BF16)
        pm = [psum([128, 512]) for _ in range(2)]
        pss = [psum([128, 512]) for _ in range(2)]
        pq3 = [psum([128, 512]) for _ in range(3)]
        pf0 = pss[0][0][:].rearrange("p (a b) -> p a b", a=4)
        pf1 = pss[1][0][:].rearrange("p (a b) -> p a b", a=4)
        pfs = [pf0, pf1]
        r_pfs = [pss[0][1], pss[1][1]]
        consts, r_consts = sb([128, 640])
        LD(lambda e: e.dma_start(out=consts[:], in_=consts_d[:, :]), w=[r_consts])
        ident = consts[:, 0:128]
        tri = consts[:, 128:256]
        maskneg = consts[:, 256:384]
        jtab = consts[:, 384:512]
        mask112 = consts[:, 512:513]
        identb, r_identb = sb([128, 128], BF16)
        V(lambda e: e.tensor_copy(out=identb[:], in_=ident), [r_consts], [r_identb])
        ones, r_ones = sb([128, 128])
        V(lambda e: e.memset(ones[:], 1.0), [], [r_ones])
        onesb, r_onesb = sb([128, 128], BF16)
        V(lambda e: e.memset(onesb[:], 1.0), [], [r_onesb])
        epsc, _r_eps = sb([128, 1])
        V(lambda e: e.memset(epsc[:], 1e-5), [], [r_ones])

        vecs, r_vecs = sb([128, 1024 + 48])
        bglu, r_bglu = sb([128, 2048], BF16)
        GD(lambda e: e.dma_start(out=bglu[:], in_=bglu_d[:, :]), w=[r_bglu])
        gfm, r_gfm = sb([128, 40])
        dDs = [sb([128, 128])] * 2
        LD(lambda e: e.dma_start(out=gfm[:], in_=gfm_d[:, :]), w=[r_gfm])
        LD(lambda e: e.dma_start(out=vecs[:], in_=vecs_d[:, :]), w=[r_vecs])
        g_mix, g_ssd, g_s5, g_mlp = 0, 1, 2, 3
        g_fin = vecs[:, 0:1024]
        b_glu = bglu
        dtb = vecs[:, 1024:1040]
        dssd = vecs[:, 1056:1072]
        arep, r_arep = sb([128, 16])
        A(lambda e: e.activation(out=arep[:], in_=vecs[:, 1040:1056], func=AF.Exp), [r_vecs], [r_arep])
        V(lambda e: e.tensor_scalar(out=arep[:], in0=arep[:], scalar1=-1.0, scalar2=None, op0=ALU.mult), [r_arep], [r_arep])

        convp, r_convp = sb([128, 60])
        LD(lambda e: e.dma_start(out=convp[:], in_=convp_d[:, :]), w=[r_convp])
        s5d, r_s5d = sb([128, 8])
        LD(lambda e: e.dma_start(out=s5d[:], in_=s5d_d[:, :]), w=[r_s5d])

        dgts = [sb([128, 128], BF16) for _ in range(2)]
        r_wscr = Res("wscr")
        aT, r_aT = sb([128, 32, 128], BF16)
        wstage = (aT[:].rearrange("p (a b) c -> p a (b c)", a=8), r_aT)

        def cast_block(blk, src, k0, c0, ncols=512, gidx=None):
            srcap = src[k0:k0 + 1024, c0:c0 + ncols].rearrange("(kc p) c -> p kc c", p=128)
            dst = wscr[blk].rearrange("p (kc c) -> p kc c", kc=8)[:, :, 0:ncols]
            stg, r_stg = wstage
            GD(lambda e: e.dma_start(out=stg[:, :, 0:ncols], in_=srcap), w=[r_stg])
            if gidx is not None:
                for kc in range(8):
                    V(lambda e, kc=kc: e.tensor_scalar(out=stg[:, kc, 0:ncols], in0=stg[:, kc, 0:ncols], scalar1=gfm[:, 8 * gidx + kc:8 * gidx + kc + 1], scalar2=None, op0=ALU.mult), [r_stg, r_gfm], [r_stg])
            LD(lambda e: e.dma_start(out=dst, in_=stg[:, :, 0:ncols]), [r_stg], [r_wscr])

        for b in range(NB_IN):
            cast_block(OFF_IN + b, w_in_d, 0, 512 * b, gidx=0)
        for b in range(NB_GLU):
            cast_block(OFF_GLU + b, w_glu_d, 0, 512 * b)
        for b in range(NB_OUT):
            cast_block(OFF_OUT + b, w_out_d, 1024 * (b // 2), 512 * (b % 2), gidx=1 + b // 2)
        for b in range(NB_UP):
            cast_block(OFF_UP + b, w_up_d, 0, 512 * b, gidx=3)
        for b in range(NB_DN):
            cast_block(OFF_DN + b, w_dn_d, 1024 * (b // 2), 512 * (b % 2))

        NWB = 4
        wbufs = [sb([128, 8, 512], BF16) for _ in range(NWB)]
        wb_i = [0]

        def load_w(blk):
            t, r = wbufs[wb_i[0] % NWB]
            wb_i[0] += 1
            LD(lambda e: e.dma_start(out=t[:].rearrange("p a c -> p (a c)"), in_=wscr[blk]), [r_wscr], [r])
            return t, r

        xh, r_xh = sb([128, 2, 4, 128])
        tt, r_tt = sb([128, 2, 4, 128])
        ww, r_ww = sb([128, 2, 4, 128])
        xtm, r_xtm = sb([128, 1024])
        yg, r_yg = sb([128, 1024])
        car, r_car = sb([128, 64])
        s5p, r_s5p = sb([128, 96])
        LD(lambda e: e.dma_start(out=s5p[:], in_=s5p_d[:, :]), w=[r_s5p])
        s5b, r_s5b = xh[:].rearrange("p a b c -> p (a b c)").rearrange("p (a b c) -> p a b c", a=32, b=2), r_xh
        LD(lambda e: e.dma_start(out=xh[:].rearrange("p a b c -> p (a b c)"), in_=s5b_d[:, :]), w=[r_s5b])
        s5c, r_s5c = tt[:].rearrange("p a b c -> p (a b c)").rearrange("p (a b c) -> p a b c", a=32, b=2), r_tt
        LD(lambda e: e.dma_start(out=tt[:].rearrange("p a b c -> p (a b c)"), in_=s5c_d[:, :]), w=[r_s5c])
        lr = s5p[:, 0:32]
        li = s5p[:, 32:64]
        sp_, r_sp = sb([128, 16, 32])
        pl = lambda i: sp_[:, i, :]
        R1 = [r_s5p, r_sp]
        A(lambda e: e.activation(out=pl(0), in_=s5p[:, 64:96], func=AF.Exp), R1, [r_sp])
        V(lambda e: e.tensor_tensor(out=pl(1), in0=lr, in1=pl(0), op=ALU.mult), R1, [r_sp])
        A(lambda e: e.activation(out=pl(2), in_=pl(1), func=AF.Exp), R1, [r_sp])
        V(lambda e: e.tensor_tensor(out=pl(3), in0=li, in1=pl(0), op=ALU.mult), R1, [r_sp])
        V(lambda e: e.tensor_scalar(out=pl(3), in0=pl(3), scalar1=1.0 / (2 * np.pi), scalar2=None, op0=ALU.mult), R1, [r_sp])

        def sincos(dst_s, dst_c, turns, rr, ww, tmp1, tmp2):
            V(lambda e: e.tensor_scalar(out=tmp1, in0=turns, scalar1=MAGIC, scalar2=MAGIC, op0=ALU.add, op1=ALU.subtract), rr, ww)
            V(lambda e: e.tensor_tensor(out=tmp1, in0=turns, in1=tmp1, op=ALU.subtract), rr, ww)
            V(lambda e: e.tensor_scalar(out=dst_s, in0=tmp1, scalar1=-0.25, scalar2=0.25, op0=ALU.max, op1=ALU.min), rr, ww)
            V(lambda e: e.scalar_tensor_tensor(out=tmp1, in0=dst_s, scalar=2.0, in1=tmp1, op0=ALU.mult, op1=ALU.subtract), rr, ww)
            A(lambda e: e.activation(out=dst_s, in_=tmp1, func=AF.Sin, scale=TWO_PI), rr, ww)
            V(lambda e: e.tensor_scalar(out=tmp2, in0=turns, scalar1=0.25, scalar2=None, op0=ALU.add), rr, ww)
            V(lambda e: e.tensor_scalar(out=tmp1, in0=tmp2, scalar1=MAGIC, scalar2=MAGIC, op0=ALU.add, op1=ALU.subtract), rr, ww)
            V(lambda e: e.tensor_tensor(out=tmp1, in0=tmp2, in1=tmp1, op=ALU.subtract), rr, ww)
            V(lambda e: e.tensor_scalar(out=dst_c, in0=tmp1, scalar1=-0.25, scalar2=0.25, op0=ALU.max, op1=ALU.min), rr, ww)
            V(lambda e: e.scalar_tensor_tensor(out=tmp1, in0=dst_c, scalar=2.0, in1=tmp1, op0=ALU.mult, op1=ALU.subtract), rr, ww)
            A(lambda e: e.activation(out=dst_c, in_=tmp1, func=AF.Sin, scale=TWO_PI), rr, ww)

        sincos(pl(4), pl(5), pl(3), R1, [r_sp], pl(14), pl(15))
        V(lambda e: e.tensor_tensor(out=pl(6), in0=pl(2), in1=pl(5), op=ALU.mult), R1, [r_sp])
        V(lambda e: e.tensor_tensor(out=pl(7), in0=pl(2), in1=pl(4), op=ALU.mult), R1, [r_sp])
        V(lambda e: e.tensor_scalar(out=pl(6), in0=pl(6), scalar1=-1.0, scalar2=None, op0=ALU.add), R1, [r_sp])
        V(lambda e: e.tensor_tensor(out=pl(8), in0=lr, in1=lr, op=ALU.mult), R1, [r_sp])
        V(lambda e: e.tensor_tensor(out=pl(9), in0=li, in1=li, op=ALU.mult), R1, [r_sp])
        V(lambda e: e.tensor_tensor(out=pl(8), in0=pl(8), in1=pl(9), op=ALU.add), R1, [r_sp])
        V(lambda e: e.reciprocal(out=pl(8), in_=pl(8)), R1, [r_sp])
        V(lambda e: e.tensor_tensor(out=pl(9), in0=pl(6), in1=lr, op=ALU.mult), R1, [r_sp])
        V(lambda e: e.tensor_tensor(out=pl(10), in0=pl(7), in1=li, op=ALU.mult), R1, [r_sp])
        V(lambda e: e.tensor_tensor(out=pl(9), in0=pl(9), in1=pl(10), op=ALU.add), R1, [r_sp])
        V(lambda e: e.tensor_tensor(out=pl(9), in0=pl(9), in1=pl(8), op=ALU.mult), R1, [r_sp])
        V(lambda e: e.tensor_tensor(out=pl(10), in0=pl(7), in1=lr, op=ALU.mult), R1, [r_sp])
        V(lambda e: e.tensor_tensor(out=pl(11), in0=pl(6), in1=li, op=ALU.mult), R1, [r_sp])
        V(lambda e: e.tensor_tensor(out=pl(10), in0=pl(10), in1=pl(11), op=ALU.subtract), R1, [r_sp])
        V(lambda e: e.tensor_tensor(out=pl(10), in0=pl(10), in1=pl(8), op=ALU.mult), R1, [r_sp])
        V(lambda e: e.tensor_scalar(out=pl(11), in0=pl(3), scalar1=128.0, scalar2=None, op0=ALU.mult), R1, [r_sp])
        sincos(pl(12), pl(13), pl(11), R1, [r_sp], pl(14), pl(15))
        rho = pl(2)
        Stab = pl(12)
        Ctab = pl(13)
        bb, r_bb = ww[:].rearrange("p a b c -> p (a b c)").rearrange("p (a b c) -> p a b c", a=2, b=32), r_ww
        tmpb, r_tmpb = xtm[:, 0:512].rearrange("p (a b) -> p a b", a=32), r_xtm
        cre_b = pl(9).unsqueeze(2).to_broadcast([128, 32, 16])
        cim_b = pl(10).unsqueeze(2).to_broadcast([128, 32, 16])
        RB = [r_sp, r_s5b, r_bb, r_tmpb]
        V(lambda e: e.tensor_tensor(out=bb[:, 0], in0=s5b[:, :, 0, :], in1=cre_b, op=ALU.mult), RB, [r_bb])
        V(lambda e: e.tensor_tensor(out=tmpb, in0=s5b[:, :, 1, :], in1=cim_b, op=ALU.mult), RB, [r_tmpb])
        V(lambda e: e.tensor_tensor(out=bb[:, 0], in0=bb[:, 0], in1=tmpb, op=ALU.subtract), RB, [r_bb])
        V(lambda e: e.tensor_tensor(out=bb[:, 1], in0=s5b[:, :, 1, :], in1=cre_b, op=ALU.mult), RB, [r_bb])
        V(lambda e: e.tensor_tensor(out=tmpb, in0=s5b[:, :, 0, :], in1=cim_b, op=ALU.mult), RB, [r_tmpb])
        V(lambda e: e.tensor_tensor(out=bb[:, 1], in0=bb[:, 1], in1=tmpb, op=ALU.add), RB, [r_bb])
        Bw, r_Bw = sb([128, 2, 32, 128], BF16)
        raccs = [sb([128, 512], BF16) for _ in range(2)]
        mpad, r_mpad = dDs[0]
        for ri in range(2):
            for pair in range(32):
                hj = pair % 4
                G(lambda e: e.memset(mpad[:], 0.0), [], [r_mpad])
                for g2 in range(2):
                    c0 = hj * 32 + g2 * 16
                    G(lambda e, g2=g2, c0=c0, ri=ri, pair=pair: e.tensor_copy(out=mpad[64 * g2:64 * g2 + 64, c0:c0 + 16], in_=bb[64 * g2:64 * g2 + 64, ri, pair, :]), [r_bb], [r_mpad])
                T(lambda e: e.transpose(pss[0][0][:, 0:128], mpad[:], ident), [r_mpad, r_consts], [pss[0][1]])
                V(lambda e, ri=ri, pair=pair: e.tensor_copy(out=Bw[:, ri, pair, :], in_=pss[0][0][:, 0:128]), [pss[0][1]], [r_Bw])
        Cw, r_Cw = sb([128, 32, 3, 64], BF16)
        V(lambda e: e.memset(Cw[:].rearrange("p a b c -> p (a b c)"), 0.0), [], [r_Cw])
        for pair in range(32):
            j = pair % 2
            for g2 in range(2):
                c0 = j * 32 + g2 * 16
                ps_ = slice(64 * g2, 64 * g2 + 64)
                V(lambda e, pair=pair, c0=c0, ps_=ps_: e.tensor_copy(out=Cw[ps_, pair, 0, c0:c0 + 16], in_=s5c[ps_, pair, 0, :]), [r_s5c], [r_Cw])
                V(lambda e, pair=pair, c0=c0, ps_=ps_: e.tensor_scalar(out=Cw[ps_, pair, 1, c0:c0 + 16], in0=s5c[ps_, pair, 0, :], scalar1=-1.0, scalar2=None, op0=ALU.mult), [r_s5c], [r_Cw])
                V(lambda e, pair=pair, c0=c0, ps_=ps_: e.tensor_scalar(out=Cw[ps_, pair, 2, c0:c0 + 16], in0=s5c[ps_, pair, 1, :], scalar1=-1.0, scalar2=None, op0=ALU.mult), [r_s5c], [r_Cw])
        cosT, r_cosT = sb([128, 32, 128])
        sinT, r_sinT = sb([128, 32, 128])
        RT = [r_xtm, r_yg, r_cosT, r_sinT]
        for q in range(4):
            for pp_ in range(8):
                pair = 8 * q + pp_
                V(lambda e, pair=pair, pp_=pp_: e.tensor_scalar(out=xtm[:, 128 * pp_:128 * (pp_ + 1)], in0=jtab, scalar1=sp_[:, 3, pair:pair + 1], scalar2=None, op0=ALU.mult), [r_consts, r_sp], [r_xtm])
            fl = lambda t, q=q: t[:, 8 * q:8 * q + 8, :].rearrange("p a b -> p (a b)")
            sincos(fl(sinT), fl(cosT), xtm[:, :], RT, RT, yg[:, :], fl(cosT))

        stopped = [False]
        try:
            stage('init', [(sp_[:].rearrange("p a b -> p (a b)"), r_sp, 512), (cosT[:, 0:4, :].rearrange("p a b -> p (a b)"), r_cosT, 512), (sinT[:, 0:4, :].rearrange("p a b -> p (a b)"), r_sinT, 512),
                           (Bw[:, 0, 0:4, :].rearrange("p a b -> p (a b)"), r_Bw, 512), (Bw[:, 1, 0:4, :].rearrange("p a b -> p (a b)"), r_Bw, 512), (Cw[:, 0:4, :, :].rearrange("p a b c -> p (a b c)"), r_Cw, 768)])
        except _Stop:
            stopped[0] = True
        Sst, r_S = sb([128, 1, 1024])
        V(lambda e: e.memset(Sst[:].rearrange("p a b -> p (a b)"), 0.0), [], [r_S])
        Sb, r_Sb = sb([128, 1024], BF16)
        V(lambda e: e.memset(Sb[:], 0.0), [], [r_Sb])
        winit, r_winit = sb([128, 2, 32])
        V(lambda e: e.memset(winit[:].rearrange("p a b -> p (a b)"), 0.0), [], [r_winit])
        zl, r_zl = sb([128, 2, 32])
        xbc_raw, r_xr = sb([128, 12, 131])
        V(lambda e: e.memset(xbc_raw[:].rearrange("p a b -> p (a b)"), 0.0), [], [r_xr])

        xt2 = [sb([128, 1024]) for _ in range(2)]
        xn, r_xn = sb([128, 1024], BF16)
        st, r_st = sb([128, 16])
        nT, r_nT = sb([128, 8, 128], BF16)
        xact, r_xact = sb([128, 8, 128])
        bc, r_bc = sb([128, 4, 128], BF16)
        cacc, r_cacc = sb([128, 128])
        cacc2, r_cacc2 = sb([128, 128])
        caccs = [(cacc, r_cacc), (cacc2, r_cacc2)]
        ubs = [sb([128, 8, 128], BF16) for _ in range(2)]
        zs, r_zs = sb([128, 1024], BF16)
        dts, r_dts = sb([128, 8, 16])
        xdt, r_xdt = sb([128, 1024], BF16)
        xds, r_xds = sb([128, 1024], BF16)
        btm, r_btm = sb([128, 256], BF16)
        eargs = [sb([128, 4, 128]) for _ in range(2)]
        ehms = [sb([128, 4, 128], BF16) for _ in range(2)]
        eareps = [sb([128, 4, 128], BF16) for _ in range(2)]
        yg2, r_yg2 = sb([128, 1024])
        ygs = [(yg, r_yg), (yg2, r_yg2)]
        hnT, r_hnT = nT, r_nT
        pr, r_pr = sb([128, 4, 4, 128], BF16)
        gels = [sb([128, 8, 128], BF16) for _ in range(2)]
        mixT, r_mixT = sb([128, 16, 128], BF16)
        hh, r_hh = sb([128, 1024])


        def rstd_of(src_ap, L, col, rsrc):
            A(lambda e: e.activation(out=xn[:L, :], in_=src_ap, func=AF.Square, accum_out=st[:L, col:col + 1]), rsrc, [r_xn, r_st])
            A(lambda e: e.activation(out=st[:L, col:col + 1], in_=st[:L, col:col + 1], func=AF.Sqrt, scale=1.0 / 1024, bias=epsc[:L, 0:1]), [r_st, r_ones], [r_st])
            V(lambda e: e.reciprocal(out=st[:L, col:col + 1], in_=st[:L, col:col + 1]), [r_st], [r_st])

        def norm_T(src_ap, rsrc, gain, L, col, dstT, kc0):
            rstd_of(src_ap, L, col, rsrc)
            A(lambda e: e.activation(out=xn[:L, :], in_=src_ap, func=AF.Identity, scale=st[:L, col:col + 1]), rsrc + [r_st], [r_xn])
            for k in range(8):
                T(lambda e, k=k: e.transpose(pT[:, 128 * k:128 * k + L], xn[:L, 128 * k:128 * (k + 1)], identb[:L, :L]), [r_xn, r_identb], [r_pT])
            A(lambda e: e.activation(out=dstT[:, kc0:kc0 + 8, :L], in_=pT[:].rearrange("p (a b) -> p a b", a=8)[:, :, :L], func=AF.Identity), [r_pT], [dstT_res[id(dstT)]])

        r_tt2 = [[Res('tt'), Res('tt')] for _ in range(2)]; r_xh2 = [[Res('xh'), Res('xh')] for _ in range(2)]; r_ww2 = [[Res('ww') for _ in range(4)] for _ in range(2)]; r_pr2 = [Res('pr0'), Res('pr1')]
        dstT_res = {id(nT): r_nT, id(mixT): r_mixT}
        r_out = Res("out")

        chunks = [(128 * i, 128) for i in range(nchunks + 1)]
        L = 128
        bc3 = lambda ap, shape, axis: ap.unsqueeze(axis).to_broadcast(shape)

        q3 = []
        deferred = []
        deferred_next = []
        pendingE = set()
        emittedE = set()
        slots_left = [20]

        eq = []

        def flushE():
            for key, f in eq:
                f()
                emittedE.add(key)
                pendingE.discard(key)
            eq[:] = []

        def pump():
            flushE()
            n = min(4, max(1, -(-len(q3) // max(slots_left[0], 1))))
            slots_left[0] = max(slots_left[0] - 1, 1)
            k = 0
            while k < n and q3:
                pc = q3[0]
                if pc['barrier'] and pendingE:
                    break
                if pc['needs'] is not None and pc['needs'] not in emittedE:
                    break
                q3.pop(0)
                pc['M']()
                k += 1
                if pc['E'] is not None:
                    eq.append((pc['key'], pc['E']))
                    pendingE.add(pc['key'])

        def flush():
            for key, f in deferred:
                f()
                if key is not None:
                    emittedE.add(key)
                    pendingE.discard(key)
            deferred[:] = deferred_next
            deferred_next[:] = []

        def dt_chain(pq, rq):
            V(lambda e: e.tensor_copy(out=dts[:L, 2, :], in_=pq[:L, 0:16]), [rq], [r_dts])
            V(lambda e: e.tensor_scalar(out=dts[:L, 3, :], in0=pq[:L, 0:16], scalar1=-1.0, scalar2=None, op0=ALU.mult), [rq], [r_dts])
            V(lambda e: e.tensor_copy(out=dts[:, 4, :], in_=pq[:, 16:32]), [rq], [r_dts])
            V(lambda e: e.tensor_tensor(out=dts[:L, 7, :], in0=dts[:L, 4, :], in1=dts[:L, 2, :], op=ALU.subtract), [r_dts], [r_dts])
            A(lambda e: e.activation(out=dts[:L, 5, :], in_=dts[:L, 7, :], func=AF.Exp), [r_dts], [r_dts])
            A(lambda e: e.activation(out=dts[:, 6, :], in_=dts[:, 4, :], func=AF.Exp), [r_dts], [r_dts])
            V(lambda e: e.tensor_tensor(out=dts[:L, 7, :], in0=dts[:L, 0, :], in1=dts[:L, 5, :], op=ALU.mult), [r_dts], [r_dts])

        def make_inproj(cj):
            t0 = chunks[cj][0]
            xt, r_xt = xt2[cj % 2]
            ub, r_ub = ubs[cj % 2]
            out = []

            def pcj(name, M, E=None, barrier=False, needs=None):
                out.append({'key': (cj, name), 'M': M, 'E': E, 'barrier': barrier, 'needs': (cj, needs) if needs is not None else None})

            def ipnorm():
                if cj + 1 < len(chunks):
                    xtn, r_xtn = xt2[(cj + 1) % 2]
                    tn = chunks[cj + 1][0]
                    LD(lambda e: e.dma_start(out=xtn[:, :], in_=x_d[tn:tn + L, :]), w=[r_xtn])
                norm_T(xt[:, :], [r_xt], g_mix, L, 0, nT, 0)
            pcj('ipnorm', ipnorm, barrier=True)
            wts = {}

            def fm_M(blk):
                wt, rw = load_w(OFF_IN + blk)
                pp, rp = pq3[blk % 3]
                for m in range(4):
                    for kc in range(8):
                        T(lambda e, m=m, kc=kc: e.matmul(pp[:, 128 * m:128 * (m + 1)], lhsT=wt[:, kc, 128 * m:128 * (m + 1)], rhs=nT[:, kc, :L], start=(kc == 0), stop=(kc == 7)), [rw, r_nT], [rp])

            def fm_E(blk):
                pp, rp = pq3[blk % 3]
                p3v = pp[:, :].rearrange("p (a b) -> p a b", a=4)
                if blk < 3:
                    A(lambda e: e.activation(out=xbc_raw[:, 4 * blk:4 * blk + 4, 3:3 + L], in_=p3v, func=AF.Identity), [rp], [r_xr])
                else:
                    A(lambda e: e.activation(out=ub[:, 4 * (blk - 3):4 * (blk - 3) + 4, :L], in_=p3v, func=AF.Identity), [rp], [r_ub])
            for blk in range(5):
                pcj('ipb%d' % blk, lambda blk=blk: fm_M(blk), lambda blk=blk: fm_E(blk), needs=('ipb%d' % (blk - 3) if blk >= 3 else None))

            def tm_M(blk):
                idx = 5 + blk
                wt, rw = load_w(OFF_IN + 5 + blk)
                pp, rp = pq3[idx % 3]
                ncol = 512 if blk < 2 else 16
                for kc in range(8):
                    T(lambda e, kc=kc: e.matmul(pp[:L, :ncol], lhsT=nT[:, kc, :L], rhs=wt[:, kc, :ncol], start=(kc == 0), stop=(kc == 7)), [rw, r_nT], [rp])

            def tm_E(blk):
                idx = 5 + blk
                pp, rp = pq3[idx % 3]
                if blk < 2:
                    A(lambda e: e.activation(out=zs[:L, 512 * blk:512 * (blk + 1)], in_=pp[:L, :], func=AF.Silu), [rp], [r_zs])
                else:
                    V(lambda e: e.tensor_tensor(out=dts[:L, 7, :], in0=pp[:L, :16], in1=dtb[:L, :], op=ALU.add), [rp, r_vecs], [r_dts])
                    A(lambda e: e.activation(out=dts[:L, 7, :], in_=dts[:L, 7, :], func=AF.Exp), [r_dts], [r_dts])
                    A(lambda e: e.activation(out=dts[:L, 0, :], in_=dts[:L, 7, :], func=AF.Ln, bias=1.0), [r_dts], [r_dts])
                    V(lambda e: e.tensor_tensor(out=dts[:L, 1, :], in0=dts[:L, 0, :], in1=arep[:L, :], op=ALU.mult), [r_dts, r_arep], [r_dts])
            for blk in range(3):
                pcj('ipt%d' % blk, lambda blk=blk: tm_M(blk), lambda blk=blk: tm_E(blk), needs='ipb%d' % (2 + blk))

            def dt_M():
                pq, rq = pq3[2]
                T(lambda e: e.matmul(pq[:L, 0:16], lhsT=tri[:L, :L], rhs=dts[:L, 1, :], start=True, stop=True), [r_consts, r_dts], [rq])
                T(lambda e: e.matmul(pq[:, 16:32], lhsT=ones[:L, :], rhs=dts[:L, 1, :], start=True, stop=True), [r_ones, r_dts], [rq])

            def dt_E():
                pq, rq = pq3[2]
                dt_chain(pq, rq)
            pcj('ipdt', dt_M, dt_E, needs='ipt2')
            return out

        def make_chunk(ci, t0):
            real = ci > 0
            xt, r_xt = xt2[ci % 2]
            yg, r_yg = ygs[ci % 2]
            gel, r_gel = gels[ci % 2]
            ub, r_ub = ubs[ci % 2]
            p1 = []
            p3 = []

            def in_proj():
                if ci == 0:
                    LD(lambda e: e.dma_start(out=xt[:, :], in_=x_d[t0:t0 + L, :]), w=[r_xt])
                if ci + 1 < len(chunks):
                    xtn, r_xtn = xt2[(ci + 1) % 2]
                    tn = chunks[ci + 1][0]
                    LD(lambda e: e.dma_start(out=xtn[:, :], in_=x_d[tn:tn + L, :]), w=[r_xtn])
                norm_T(xt[:, :], [r_xt], g_mix, L, 0, nT, 0)
                stage('norm', [(nT[:].rearrange("p a b -> p (a b)"), r_nT, 1024), (st[:, :], r_st, 16), (xt[:, :], r_xt, 1024)], ci)
                for blk in range(5):
                    wt, rw = load_w(OFF_IN + blk)
                    for m in range(4):
                        mt = blk * 4 + m
                        pp, rp = pm[mt % 2]
                        for kc in range(8):
                            T(lambda e, pp=pp, wt=wt, m=m, kc=kc: e.matmul(pp[:, :L], lhsT=wt[:, kc, 128 * m:128 * (m + 1)], rhs=nT[:, kc, :L], start=(kc == 0), stop=(kc == 7)), [rw, r_nT], [rp])
                        if mt < 12:
                            V(lambda e, pp=pp, mt=mt: e.tensor_copy(out=xbc_raw[:, mt, 3:3 + L], in_=pp[:, :L]), [rp], [r_xr])
                        else:
                            V(lambda e, pp=pp, mt=mt: e.tensor_copy(out=ub[:, mt - 12, :L], in_=pp[:, :L]), [rp], [r_ub])
                for blk in range(3):
                    wt, rw = load_w(OFF_IN + 5 + blk)
                    pp, rp = pm[blk % 2]
                    ncol = 512 if blk < 2 else 16
                    for kc in range(8):
                        T(lambda e, pp=pp, wt=wt, kc=kc, ncol=ncol: e.matmul(pp[:L, :ncol], lhsT=nT[:, kc, :L], rhs=wt[:, kc, :ncol], start=(kc == 0), stop=(kc == 7)), [rw, r_nT], [rp])
                    if blk < 2:
                        A(lambda e, pp=pp, blk=blk: e.activation(out=zs[:L, 512 * blk:512 * (blk + 1)], in_=pp[:L, :], func=AF.Silu), [rp], [r_zs])
                    else:
                        V(lambda e, pp=pp: e.tensor_tensor(out=dts[:L, 7, :], in0=pp[:L, :16], in1=dtb[:L, :], op=ALU.add), [rp, r_vecs], [r_dts])
                        A(lambda e: e.activation(out=dts[:L, 7, :], in_=dts[:L, 7, :], func=AF.Exp), [r_dts], [r_dts])
                        A(lambda e: e.activation(out=dts[:L, 0, :], in_=dts[:L, 7, :], func=AF.Ln, bias=1.0), [r_dts], [r_dts])
                        if not real:
                            V(lambda e: e.tensor_scalar(out=dts[:, 0, :], in0=dts[:, 0, :], scalar1=mask112, scalar2=None, op0=ALU.mult), [r_dts, r_consts], [r_dts])
                        V(lambda e: e.tensor_tensor(out=dts[:L, 1, :], in0=dts[:L, 0, :], in1=arep[:L, :], op=ALU.mult), [r_dts, r_arep], [r_dts])
                stage('proj', [(nT[:].rearrange("p a b -> p (a b)"), r_nT, 1024), (xbc_raw[:, :, 3:131], r_xr, 1536), (ub[:].rearrange("p a b -> p (a b)"), r_ub, 1024), (zs[:, :], r_zs, 1024), (dts[:].rearrange("p a b -> p (a b)"), r_dts, 128), (st[:, :], r_st, 16)], ci)
            if ci == 0:
                p1.append(in_proj)

            def conv():
                for mt in range(12):
                    ca, r_ca = caccs[mt % 2]
                    cw = lambda k, mt=mt: convp[:, 5 * mt + k:5 * mt + k + 1]
                    V(lambda e, mt=mt, cw=cw, ca=ca: e.tensor_scalar(out=ca[:, :L], in0=xbc_raw[:, mt, 0:L], scalar1=cw(0), scalar2=None, op0=ALU.mult), [r_xr, r_convp], [r_ca])
                    for k in range(1, 4):
                        V(lambda e, mt=mt, cw=cw, k=k, ca=ca: e.scalar_tensor_tensor(out=ca[:, :L], in0=xbc_raw[:, mt, k:k + L], scalar=cw(k), in1=ca[:, :L], op0=ALU.mult, op1=ALU.add), [r_xr, r_convp, r_ca], [r_ca])
                    if mt < 8:
                        A(lambda e, mt=mt, cw=cw, ca=ca: e.activation(out=xact[:, mt, :L], in_=ca[:, :L], func=AF.Silu, bias=cw(4)), [r_ca, r_convp], [r_xact])
                    else:
                        A(lambda e, mt=mt, cw=cw, ca=ca: e.activation(out=bc[:, mt - 8, :L], in_=ca[:, :L], func=AF.Silu, bias=cw(4)), [r_ca, r_convp], [r_bc])
                for mt in range(12):
                    G(lambda e, mt=mt: e.tensor_copy(out=xbc_raw[:, mt, 0:3], in_=xbc_raw[:, mt, L:L + 3]), [r_xr], [r_xr])
                stage('conv', [(xact[:].rearrange("p a b -> p (a b)"), r_xact, 1024), (bc[:].rearrange("p a b -> p (a b)"), r_bc, 512)], ci)
            p1.append(lambda: (pump(), conv(), flush()))

            def ssd_pre():
                pp, rp = pss[0]
                for k in range(8):
                    T(lambda e, pp=pp, k=k: e.transpose(pp[:L, 128 * (k % 4):128 * (k % 4 + 1)], xact[:, k, :L], ident), [r_xact, r_consts], [rp])
                    if k % 4 == 3:
                        A(lambda e, pp=pp, k=k: e.activation(out=xtm[:L, 512 * (k // 4):512 * (k // 4 + 1)], in_=pp[:L, :], func=AF.Identity), [rp], [r_xtm])
                for g in range(2):
                    T(lambda e, g=g: e.transpose(pT[:L, 128 * g:128 * (g + 1)], bc[:, g, :L], identb[:, :]), [r_bc, r_identb], [r_pT])
                A(lambda e: e.activation(out=btm[:L, :], in_=pT[:L, 0:256], func=AF.Identity), [r_pT], [r_btm])
                if ci == 0:
                    pq, rq = pss[1]
                    T(lambda e: e.matmul(pq[:L, 0:16], lhsT=tri[:L, :L], rhs=dts[:L, 1, :], start=True, stop=True), [r_consts, r_dts], [rq])
                    T(lambda e: e.matmul(pq[:, 16:32], lhsT=ones[:L, :], rhs=dts[:L, 1, :], start=True, stop=True), [r_ones, r_dts], [rq])
                    dt_chain(pq, rq)
                x3 = xtm[:, :].rearrange("p (h c) -> p h c", h=16)
                V(lambda e: e.tensor_tensor(out=xdt[:, :].rearrange("p (h c) -> p h c", h=16), in0=x3, in1=bc3(dts[:, 0, :], [128, 16, 64], 2), op=ALU.mult), [r_xtm, r_dts], [r_xdt])
                V(lambda e: e.tensor_tensor(out=xds[:, :].rearrange("p (h c) -> p h c", h=16), in0=x3, in1=bc3(dts[:, 7, :], [128, 16, 64], 2), op=ALU.mult), [r_xtm, r_dts], [r_xds])
            p1.append(lambda: (ssd_pre(), flush()))

            def ssd_arow(g):
                for q4 in range(2):
                    h0 = 8 * g + 4 * q4
                    pa, ra = pss[q4]
                    pa3 = pa[:].rearrange("p (a b) -> p a b", a=4)
                    for i4 in range(4):
                        h = h0 + i4
                        T(lambda e, i4=i4, h=h, pa3=pa3: e.matmul(pa3[:, i4, :], lhsT=dts[:L, 1, h:h + 1].to_broadcast([L, 128]), rhs=tri[:L, :L], start=True, stop=True), [r_dts, r_consts], [ra])

            def ssd_group(g):
                py, ry = pm[g]
                po, ro = pm[1 - g]
                pcb = po[:, 0:128]
                gs = slice(512 * g, 512 * (g + 1))
                T(lambda e: e.matmul(pcb, lhsT=bc[:, g, :L], rhs=bc[:, 2 + g, :L], start=True, stop=True), [r_bc], [ro])
                if g == 0:
                    ssd_arow(0)
                pump()
                for q4 in range(2):
                    h0 = 8 * g + 4 * q4
                    pa, ra = pss[q4]
                    pa3 = pa[:].rearrange("p (a b) -> p a b", a=4)
                    earg, r_earg = eargs[q4]
                    V(lambda e, earg=earg, pa3=pa3: e.tensor_tensor(out=earg[:], in0=pa3, in1=bc3(maskneg, [128, 4, 128], 1), op=ALU.add), [ra, r_consts], [r_earg])
                    V(lambda e, earg=earg, h0=h0: e.tensor_tensor(out=earg[:], in0=earg[:], in1=bc3(dts[:, 3, h0:h0 + 4], [128, 4, 128], 2), op=ALU.add), [r_earg, r_dts], [r_earg])
                for q4 in range(2):
                    pa, ra = pss[q4]
                    pa3 = pa[:].rearrange("p (a b) -> p a b", a=4)
                    earg, r_earg = eargs[q4]
                    ehm, r_ehm = ehms[q4]
                    earep, r_earep = eareps[q4]
                    A(lambda e, ehm=ehm, earg=earg: e.activation(out=ehm[:], in_=earg[:], func=AF.Exp), [r_earg], [r_ehm])
                    A(lambda e, earep=earep, pa3=pa3: e.activation(out=earep[:], in_=pa3, func=AF.Exp), [ra], [r_earep])
                for q4 in range(2):
                    ehm, r_ehm = ehms[q4]
                    earep, r_earep = eareps[q4]
                    V(lambda e, ehm=ehm: e.tensor_tensor(out=ehm[:], in0=ehm[:], in1=bc3(pcb, [128, 4, 128], 1), op=ALU.mult), [ro, r_ehm], [r_ehm])
                    G(lambda e, earep=earep: e.tensor_tensor(out=earep[:], in0=earep[:], in1=bc3(bc[:, 2 + g, :], [128, 4, 128], 1), op=ALU.mult), [r_bc, r_earep], [r_earep])
                for q4 in range(2):
                    h0 = 8 * g + 4 * q4
                    ehm, r_ehm = ehms[q4]
                    earep, r_earep = eareps[q4]
                    for i4 in range(4):
                        h = h0 + i4
                        hl = 4 * q4 + i4
                        hs = slice(64 * h, 64 * (h + 1))
                        tl = 4 * g + hl // 2
                        dD, r_dD = dDs[tl % 2]
                        if hl % 2 == 0:
                            A(lambda e, dD=dD, tl=tl: e.activation(out=dD[:, :], in_=ident, func=AF.Identity, scale=gfm[:, 32 + tl:33 + tl]), [r_consts, r_gfm], [r_dD])
                        T(lambda e, dD=dD, tl=tl, hl=hl: e.matmul(py[:L, 64 * hl:64 * (hl + 1)], lhsT=xact[:, tl, :], rhs=dD[:, 64 * (hl % 2):64 * (hl % 2) + 64], start=True, stop=False), [r_xact, r_dD], [ry])
                        T(lambda e, ehm=ehm, i4=i4, hl=hl, hs=hs: e.matmul(py[:L, 64 * hl:64 * (hl + 1)], lhsT=ehm[:, i4, :], rhs=xdt[:L, hs], start=False, stop=False), [r_ehm, r_xdt], [ry])
                        T(lambda e, earep=earep, i4=i4, hl=hl, hs=hs: e.matmul(py[:L, 64 * hl:64 * (hl + 1)], lhsT=earep[:, i4, :], rhs=Sb[:, hs], start=False, stop=True), [r_earep, r_Sb], [ry])
                if g == 0:
                    ssd_arow(1)
                g3 = lambda t: t[:, gs].rearrange("p (h c) -> p h c", h=8)
                V(lambda e: e.tensor_tensor(out=yg[:, gs], in0=py[:, :], in1=zs[:, gs], op=ALU.mult), [ry, r_zs], [r_yg])
                T(lambda e: e.matmul(po[:, :], lhsT=btm[:L, 128 * g:128 * (g + 1)], rhs=xds[:L, gs], start=True, stop=True), [r_btm, r_xds], [ro])
                s3 = Sst[:, 0, gs].rearrange("p (h c) -> p h c", h=8)
                V(lambda e: e.tensor_tensor(out=s3, in0=s3, in1=bc3(dts[:, 6, 8 * g:8 * g + 8], [128, 8, 64], 2), op=ALU.mult), [r_S, r_dts], [r_S])
                V(lambda e: e.tensor_tensor(out=Sst[:, 0, gs], in0=Sst[:, 0, gs], in1=po[:, :], op=ALU.add), [r_S, ro], [r_S])
                if g == 1:
                    V(lambda e: e.tensor_copy(out=Sb[:, :], in_=Sst[:, 0, :]), [r_S], [r_Sb])
                    stage('ssd', [(yg[:, :], r_yg, 1024), (xtm[:, :], r_xtm, 1024), (dts[:].rearrange("p a b -> p (a b)"), r_dts, 128), (Sst[:, 0, :], r_S, 1024)], ci)
            p1.append(lambda: (ssd_group(0), flush()))
            p1.append(lambda: (ssd_group(1), flush()))

            first = (ci == 0)

            def s5_view(u):
                t, hh2 = divmod(u, 2)
                b = u % 2
                pz, rz = pss[b]
                pz4 = pz[:].rearrange("p (r j b) -> p r j b", r=2, j=2)
                p0 = 4 * t + 2 * hh2
                return t, hh2, b, pz4, rz, p0, slice(2 * b, 2 * b + 2)

            def s5_A(u):
                t, hh2, b, pz4, rz, p0, js = s5_view(u)
                for j in range(2):
                    for ri in range(2):
                        T(lambda e, j=j, ri=ri: e.matmul(pz4[:, ri, j, :], lhsT=Bw[:, ri, p0 + j, :], rhs=ub[:, t, :], start=True, stop=True), [r_Bw, r_ub], [rz])

            def s5_mults(u):
                t, hh2, b, pz4, rz, p0, js = s5_view(u)
                cs = cosT[:, p0:p0 + 2, :]
                sn = sinT[:, p0:p0 + 2, :]
                old = [r_tt, r_xh, r_ww, r_pr] if first and u < 2 else []
                r_t, r_x = r_tt2[b], r_xh2[b]
                RR = [rz, r_cosT, r_sinT]
                V(lambda e: e.tensor_tensor(out=xh[:, 0, js, :], in0=pz4[:, 0], in1=cs, op=ALU.mult), RR, [r_x[0]] + old)
                V(lambda e: e.tensor_tensor(out=tt[:, 0, js, :], in0=pz4[:, 1], in1=sn, op=ALU.mult), RR, [r_t[0]])
                V(lambda e: e.tensor_tensor(out=xh[:, 1, js, :], in0=pz4[:, 1], in1=cs, op=ALU.mult), RR, [r_x[1]])
                V(lambda e: e.tensor_tensor(out=tt[:, 1, js, :], in0=pz4[:, 0], in1=sn, op=ALU.mult), RR, [r_t[1]])
                G(lambda e: e.tensor_tensor(out=xh[:, 0, js, :], in0=xh[:, 0, js, :], in1=tt[:, 0, js, :], op=ALU.add), [r_x[0], r_t[0]], [r_x[0]])
                G(lambda e: e.tensor_tensor(out=xh[:, 1, js, :], in0=xh[:, 1, js, :], in1=tt[:, 1, js, :], op=ALU.subtract), [r_x[1], r_t[1]], [r_x[1]])

            def s5_unit(u):
                if u == 0:
                    s5_A(0)
                    s5_A(1)
                    s5_mults(0)
                if u + 2 < 16:
                    s5_A(u + 2)
                pump()
                if u + 1 < 16:
                    s5_mults(u + 1)
                t, hh2, b, pz4, rz, p0, js = s5_view(u)
                cs = cosT[:, p0:p0 + 2, :]
                sn = sinT[:, p0:p0 + 2, :]
                r_x, r_w, r_p = r_xh2[b], r_ww2[b], r_pr2[b]
                for j in range(2):
                    pair = p0 + j
                    for ri in range(2):
                        V(lambda e, j=j, pair=pair, ri=ri: e.tensor_tensor_scan(out=ww[:, ri, 2 * b + j, :], data0=sp_[:, 2, pair:pair + 1].to_broadcast([128, L]), data1=xh[:, ri, 2 * b + j, :], initial=winit[:, ri, pair:pair + 1], op0=ALU.mult, op1=ALU.add), [r_sp, r_x[ri], r_winit], [r_w[2 * ri + j]])
                V(lambda e: e.tensor_copy(out=zl[:, :, p0:p0 + 2], in_=ww[:, :, js, L - 1]), r_w, [r_zl])
                flush()
                RP = r_w + [r_cosT, r_sinT]
                G(lambda e: e.tensor_tensor(out=pr[:, 0, js, :], in0=ww[:, 0, js, :], in1=cs, op=ALU.mult), RP, [r_p])
                G(lambda e: e.tensor_tensor(out=pr[:, 1, js, :], in0=ww[:, 1, js, :], in1=sn, op=ALU.mult), RP, [r_p])
                G(lambda e: e.tensor_tensor(out=pr[:, 2, js, :], in0=ww[:, 1, js, :], in1=cs, op=ALU.mult), RP, [r_p])
                G(lambda e: e.tensor_tensor(out=pr[:, 3, js, :], in0=ww[:, 0, js, :], in1=sn, op=ALU.mult), RP, [r_p])
                py, ry = pm[t % 2]
                n = 0
                for j in range(2):
                    pair = p0 + j
                    for var, wv in ((0, 0), (1, 1), (2, 2), (3, 2)):
                        T(lambda e, pair=pair, var=var, wv=wv, j=j, n=n: e.matmul(py[64 * hh2:64 * hh2 + 64, :L], lhsT=Cw[:, pair, wv, :], rhs=pr[:, var, 2 * b + j, :], start=(n == 0), stop=False), [r_Cw, r_p], [ry])
                        n += 1
                dgt, r_dgt = dgts[t % 2]
                if hh2 == 0:
                    A(lambda e: e.activation(out=dgt[:, :], in_=identb[:, :], func=AF.Identity, scale=s5d[:, t:t + 1]), [r_identb, r_s5d], [r_dgt])
                T(lambda e: e.matmul(py[64 * hh2:64 * hh2 + 64, :L], lhsT=dgt[:, 64 * hh2:64 * hh2 + 64], rhs=ub[:, t, :L], start=False, stop=True), [r_dgt, r_ub], [ry])
                if hh2 == 1:
                    def epi():
                        A(lambda e: e.activation(out=gel[:, t, :L], in_=py[:, :L], func=AF.Gelu), [ry], [r_gel])
                    if u == 15:
                        flush()
                        epi()
                    else:
                        deferred_next.append((None, epi))
                if u == 15:
                    stage('s5', [(gel[:].rearrange("p a b -> p (a b)"), r_gel, 1024), (gel[:].rearrange("p a b -> p (a b)"), r_gel, 1024), (zl[:].rearrange("p a b -> p (a b)"), r_zl, 64)], ci)
                    RC = [r_zl, r_sp, r_winit, r_car]
                    tA = car[:, 0:32]
                    tB = car[:, 32:64]
                    V(lambda e: e.tensor_tensor(out=tA, in0=zl[:, 0, :], in1=Ctab, op=ALU.mult), RC, [r_car])
                    V(lambda e: e.tensor_tensor(out=tB, in0=zl[:, 1, :], in1=Stab, op=ALU.mult), RC, [r_car])
                    V(lambda e: e.tensor_tensor(out=winit[:, 0, :], in0=tA, in1=tB, op=ALU.subtract), RC, [r_winit])
                    V(lambda e: e.tensor_tensor(out=tA, in0=zl[:, 1, :], in1=Ctab, op=ALU.mult), RC, [r_car])
                    V(lambda e: e.tensor_tensor(out=tB, in0=zl[:, 0, :], in1=Stab, op=ALU.mult), RC, [r_car])
                    V(lambda e: e.tensor_tensor(out=winit[:, 1, :], in0=tA, in1=tB, op=ALU.add), RC, [r_winit])
            if ci + 1 < len(chunks):
                p1.append(lambda: q3.extend(make_inproj(ci + 1)))
            for u in range(16):
                p1.append(lambda u=u: s5_unit(u))

            if not real:
                return p1, p3

            def pc(name, M, E=None, barrier=False, needs=None):
                p3.append({'key': (ci, name), 'M': M, 'E': E, 'barrier': barrier, 'needs': (ci, needs) if needs is not None else None})

            pc('nssd', lambda: norm_T(yg[:, :], [r_yg], g_ssd, 128, 1, mixT, 0))

            def glu_M(jh):
                banks = [pq3[0], pq3[1]] if jh == 0 else [pq3[2], pq3[0]]
                for idx, blk in enumerate((jh, jh + 2)):
                    wt, rw = load_w(OFF_GLU + blk)
                    pp, rp = banks[idx]
                    for kc in range(8):
                        T(lambda e, kc=kc, pp=pp, wt=wt: e.matmul(pp[:, :], lhsT=gel[:, kc, :], rhs=wt[:, kc, :], start=(kc == 0), stop=False), [rw, r_gel], [rp])
                    T(lambda e, pp=pp, blk=blk: e.matmul(pp[:, :], lhsT=onesb[0:1, :], rhs=bglu[0:1, 512 * blk:512 * (blk + 1)], start=False, stop=True), [r_onesb, r_bglu], [rp])

            def glu_E(jh):
                banks = [pq3[0], pq3[1]] if jh == 0 else [pq3[2], pq3[0]]
                (p1_, r1_), (p2_, r2_) = banks
                sg, r_sg = raccs[jh]
                A(lambda e: e.activation(out=sg[:, :], in_=p2_[:, :], func=AF.Sigmoid), [r2_], [r_sg])
                V(lambda e: e.tensor_tensor(out=hh[:, 512 * jh:512 * (jh + 1)], in0=p1_[:, :], in1=sg[:, :], op=ALU.mult), [r1_, r_sg], [r_hh])
            pc('gluA', lambda: glu_M(0), lambda: glu_E(0), barrier=True)
            pc('gluB', lambda: glu_M(1), lambda: glu_E(1), needs='gluA')

            def glu_post():
                norm_T(hh[:, :], [r_hh], g_s5, 128, 2, mixT, 8)
            pc('glupost', glu_post, barrier=True)

            def out_M(ch, kh):
                pp, rp = pq3[ch]
                if ch == 0 and kh == 0:
                    LD(lambda e: e.dma_start(out=hh[:, :], in_=x_d[t0:t0 + L, :]), w=[r_hh])
                wt, rw = load_w(OFF_OUT + 2 * kh + ch)
                for kc in range(8):
                    T(lambda e, kc=kc: e.matmul(pp[:, :], lhsT=mixT[:, 8 * kh + kc, :], rhs=wt[:, kc, :], start=(kh == 0 and kc == 0), stop=False), [rw, r_mixT], [rp])
                if kh == 1:
                    T(lambda e: e.matmul(pp[:, :], lhsT=ident, rhs=hh[:, 512 * ch:512 * (ch + 1)], start=False, stop=True), [r_consts, r_hh], [rp])

            def out_E(ch):
                pp, rp = pq3[ch]
                A(lambda e: e.activation(out=hh[:, 512 * ch:512 * (ch + 1)], in_=pp[:, :], func=AF.Identity), [rp], [r_hh])
            for ch in range(2):
                for kh in range(2):
                    pc('out%d%d' % (ch, kh), lambda ch=ch, kh=kh: out_M(ch, kh), (lambda ch=ch: out_E(ch)) if kh == 1 else None)

            def mlp_norm():
                stage('mix', [(hh[:, :], r_hh, 1024), (mixT[:].rearrange("p a b -> p (a b)"), r_mixT, 2048)], ci)
                norm_T(hh[:, :], [r_hh], g_mlp, 128, 3, hnT, 0)
            pc('mlpnorm', mlp_norm, barrier=True)

            wts = {}

            def up_M(blk):
                wt, rw = load_w(OFF_UP + blk)
                pp, rp = pq3[blk % 3]
                for m in range(4):
                    for kc in range(8):
                        T(lambda e, m=m, kc=kc: e.matmul(pp[:, 128 * m:128 * (m + 1)], lhsT=wt[:, kc, 128 * m:128 * (m + 1)], rhs=hnT[:, kc, :], start=(kc == 0), stop=(kc == 7)), [rw, r_hnT], [rp])

            def up_E(blk):
                pp, rp = pq3[blk % 3]
                rc, r_rc = raccs[blk % 2]
                A(lambda e: e.activation(out=rc[:, :], in_=pp[:, :], func=AF.Relu), [rp], [r_rc])
                G(lambda e: e.tensor_tensor(out=aT[:, 4 * blk:4 * blk + 4, :], in0=rc[:, :].rearrange("p (a b) -> p a b", a=4), in1=rc[:, :].rearrange("p (a b) -> p a b", a=4), op=ALU.mult), [r_rc], [r_aT])
            for blk in range(8):
                pc('up%d' % blk, lambda blk=blk: up_M(blk), lambda blk=blk: up_E(blk), needs=('up%d' % (blk - 3) if blk >= 3 else None))

            def dn_M(ch, kq):
                pp, rp = pq3[ch]
                wt, rw = load_w(OFF_DN + 2 * kq + ch)
                for kc in range(8):
                    T(lambda e, kc=kc: e.matmul(pp[:, :], lhsT=aT[:, 8 * kq + kc, :], rhs=wt[:, kc, :], start=(kq == 0 and kc == 0), stop=False), [rw, r_aT], [rp])
                if kq == 3:
                    T(lambda e: e.matmul(pp[:, :], lhsT=ident, rhs=hh[:, 512 * ch:512 * (ch + 1)], start=False, stop=True), [r_consts, r_hh], [rp])

            def dn_E(ch):
                pp, rp = pq3[ch]
                A(lambda e: e.activation(out=hh[:, 512 * ch:512 * (ch + 1)], in_=pp[:, :], func=AF.Identity), [rp], [r_hh])
            for ch in range(2):
                for kq in range(4):
                    pc('dn%d%d' % (ch, kq), lambda ch=ch, kq=kq: dn_M(ch, kq), (lambda ch=ch: dn_E(ch)) if kq == 3 else None, barrier=(ch == 0 and kq == 0))

            def final():
                rstd_of(hh[:, :], 128, 4, [r_hh])
                V(lambda e: e.scalar_tensor_tensor(out=yg[:, :], in0=hh[:, :], scalar=st[:, 4:5], in1=g_fin, op0=ALU.mult, op1=ALU.mult), [r_hh, r_st, r_vecs], [r_yg])
                GD(lambda e: e.dma_start(out=out_d[128 * (ci - 1):128 * ci, :], in_=yg[:, :]), [r_yg], [r_out])
            pc('final', final, barrier=True)
            return p1, p3

        try:
            for ci, (t0, _L) in enumerate(chunks):
                if stopped[0]:
                    break
                guard = 0
                stale = lambda k: k is not None and (k[0] <= ci - 2 or (k[0] == ci and k[1].startswith('ip')))
                while guard < 1000 and (any(stale(p['key']) for p in q3) or any(stale(k) for k, _ in deferred + deferred_next + eq)):
                    slots_left[0] = 1
                    pump()
                    flush()
                    guard += 1
                p1, p3 = make_chunk(ci, t0)
                slots_left[0] = 20
                for f in p1:
                    f()
                if stop is not None:
                    for pcd in p3:
                        pcd['M']()
                        if pcd['E'] is not None:
                            pcd['E']()
                else:
                    q3.extend(p3)
            guard = 0
            while (q3 or deferred or deferred_next or eq) and guard < 1000:
                slots_left[0] = 1
                pump()
                flush()
                guard += 1
        except _Stop:
            pass
        P.wait_all('pool', [r_out, r_dbg])
        P.emit()
    return nc


def _prep_inputs(inp, b, nchunks=NCH):
    f = lambda a: np.ascontiguousarray(np.asarray(a, dtype=np.float32))
    x = f(inp['x'])[b]
    meta = f(inp['meta_tokens'])
    xs = np.concatenate([np.zeros((112, 1024), np.float32), meta, x[:128 * nchunks]], axis=0)
    w_in = f(inp['w_in'])[0]
    o_xbc, o_dt, o_u = 1024, 1024 + 1536, 1024 + 1536 + 16
    w_in_p = np.zeros((1024, 4096), np.float32)
    w_in_p[:, 0:1536] = w_in[:, o_xbc:o_dt]
    w_in_p[:, 1536:2560] = w_in[:, o_u:]
    w_in_p[:, 2560:3584] = w_in[:, 0:1024]
    w_in_p[:, 3584:3600] = w_in[:, o_dt:o_u]
    gfm = np.concatenate([f(inp[k])[0].reshape(8, 128).T for k in ('g_mix', 'g_ssd', 'g_s5', 'g_mlp')] + [np.repeat(f(inp['d_ssd'])[0], 64).reshape(8, 128).T], axis=1)
    vecs = np.concatenate([f(inp['g_final']), f(inp['dt_bias'])[0], f(inp['a_log'])[0], f(inp['d_ssd'])[0]])[None, :].repeat(128, axis=0)
    bglu = f(inp['b_glu'])[0][None, :].repeat(128, axis=0)
    cw = f(inp['conv_w'])[0]
    cb = f(inp['conv_b'])[0]
    cp = np.concatenate([cw, cb[None, :]], axis=0)
    convp = cp.reshape(5, 12, 128).transpose(2, 1, 0).reshape(128, 60)
    ii = np.arange(128)
    ident = np.eye(128, dtype=np.float32)
    tri = (ii[:, None] <= ii[None, :]).astype(np.float32)
    maskneg = np.where(ii[None, :] < ii[:, None], -30000.0, 0.0).astype(np.float32)
    jtab = np.broadcast_to((ii - 127).astype(np.float32)[None, :], (128, 128))
    m112 = np.zeros((128, 128), np.float32); m112[112:, :] = 1.0
    consts = np.concatenate([ident, tri, maskneg, jtab, m112], axis=1)
    gp = lambda a: f(a).reshape(32, 2, 64).transpose(1, 2, 0).reshape(128, 32)
    ls = np.repeat(f(inp['log_step'])[0][:, None], 64, axis=1)
    s5p = np.concatenate([gp(f(inp['lam_re'])[0]), gp(f(inp['lam_im'])[0]), gp(ls)], axis=1)
    gph = lambda a: a.reshape(32, 2, 64, 16).transpose(1, 2, 0, 3).reshape(128, 32, 16)
    bre = gph(f(inp['b_re'])[0]); bim = gph(f(inp['b_im'])[0])
    s5b = np.stack([bre, bim], axis=2).reshape(128, 32 * 2 * 16)
    cre = gph(f(inp['c_re'])[0].transpose(0, 2, 1)); cim = gph(f(inp['c_im'])[0].transpose(0, 2, 1))
    s5c = np.stack([cre, cim], axis=2).reshape(128, 32 * 2 * 16)
    s5d = f(inp['d_s5'])[0].reshape(8, 128).T
    m = {
        'x': xs, 'w_in': w_in_p, 'w_glu': f(inp['w_glu'])[0], 'w_out': f(inp['w_out'])[0],
        'w_up': f(inp['w_up'])[0], 'w_down': f(inp['w_down'])[0], 'vecs': vecs, 'bglu': bglu, 'convp': convp,
        'consts': consts, 'gfm': gfm, 's5p': s5p, 's5b': s5b, 's5c': s5c, 's5d': s5d,
    }
    return {k: np.ascontiguousarray(v, dtype=np.float32) for k, v in m.items()}


def kernel(**inputs):
    nc = build(NCH)
    maps = [_prep_inputs(inputs, 0), _prep_inputs(inputs, 1)]
    in_maps = [maps[c % 2] for c in range(8)]
    res = run_bass_kernel_spmd(nc, in_maps, core_ids=list(range(8)))
    out = np.stack([res.results[0]['out'], res.results[1]['out']], axis=0)
    return out.astype(np.float32)
```

```python
import numpy as np
from contextlib import ExitStack
import concourse.bass as bass
import concourse.mybir as mybir
from concourse.bass_utils import run_bass_kernel_spmd

F32 = mybir.dt.float32
BF16 = mybir.dt.bfloat16
AF = mybir.ActivationFunctionType
ALU = mybir.AluOpType

D = 1024
SEQ = 8192
NMETA = 16
NCH = SEQ // 128
MAGIC = 12582912.0
TWO_PI = 6.283185


class Res:
    def __init__(self, name):
        self.name = name
        self.lw = None
        self.rd = []
        self.dsem = None
        self.dcnt = 0


class Prog:
    ENG = ('pe', 'act', 'dve', 'pool', 'sp')

    def __init__(self, nc, es):
        self.nc = nc
        self.es = es
        self.q = {e: [] for e in self.ENG}
        self.cnt = {e: 0 for e in self.ENG}
        self.sem = {e: es.enter_context(nc.semaphore('s_' + e)) for e in self.ENG}
        self.waited = {e: {} for e in self.ENG}
        self.nd = 0

    def _need(self, eng, tok, waits):
        if tok is None:
            return
        if tok[0] == 'e':
            _, e2, idx = tok
            if e2 == eng and eng == 'pe':
                return
            key = ('e', e2)
            val = idx
            sem = self.sem[e2]
        else:
            r = tok[1]
            key = ('d', id(r))
            val = r.dcnt
            sem = r.dsem
        if self.waited[eng].get(key, 0) >= val:
            return
        self.waited[eng][key] = val
        waits.append((sem, val))

    def op(self, eng, fn, reads=(), writes=(), dma=False):
        waits = []
        for r in reads:
            self._need(eng, r.lw, waits)
        for w in writes:
            self._need(eng, w.lw, waits)
            for t in w.rd:
                self._need(eng, t, waits)
        if dma:
            w = writes[0]
            if w.dsem is None:
                w.dsem = self.es.enter_context(self.nc.semaphore('d_%d' % self.nd))
                self.nd += 1
            w.dcnt += 16
            tok = ('d', w)
            inc = (w.dsem, 16)
        else:
            self.cnt[eng] += 1
            tok = ('e', eng, self.cnt[eng])
            inc = (self.sem[eng], 1)
        for r in reads:
            r.rd.append(tok)
            if len(r.rd) > 64:
                r.rd = r.rd[-64:]
        for w in writes:
            w.lw = tok
            w.rd = []
        self.q[eng].append((waits, fn, inc))

    def wait_all(self, eng, ress):
        waits = []
        for r in ress:
            self._need(eng, r.lw, waits)
        self.q[eng].append((waits, None, None))

    def emit(self):
        nc = self.nc
        with nc.Block() as block:
            def run(e, handle):
                for waits, fn, inc in self.q[e]:
                    for sem, val in waits:
                        handle.wait_ge(sem, val)
                    if fn is not None:
                        fn(handle).then_inc(inc[0], inc[1])

            @block.tensor
            def _(h):
                run('pe', h)

            @block.scalar
            def _(h):
                run('act', h)

            @block.vector
            def _(h):
                run('dve', h)

            @block.gpsimd
            def _(h):
                run('pool', h)

            @block.sync
            def _(h):
                run('sp', h)


NB_IN, NB_GLU, NB_OUT, NB_UP, NB_DN = 8, 4, 4, 8, 8
OFF_IN = 0
OFF_GLU = OFF_IN + NB_IN
OFF_OUT = OFF_GLU + NB_GLU
OFF_UP = OFF_OUT + NB_OUT
OFF_DN = OFF_UP + NB_UP
NBLK = OFF_DN + NB_DN


class _Stop(Exception):
    pass


def build(nchunks=NCH, stop=None, stop_chunk=0):
    nc = bass.Bass("TRN2", target_bir_lowering=False)
    dram = lambda n, s, dt=F32, kind="ExternalInput": nc.dram_tensor(n, s, dt, kind=kind).ap()
    NT = 128 + 128 * nchunks
    x_d = dram("x", [NT, D])
    w_in_d = dram("w_in", [D, 4096])
    w_glu_d = dram("w_glu", [D, 2048])
    w_out_d = dram("w_out", [2048, D])
    w_up_d = dram("w_up", [D, 4096])
    w_dn_d = dram("w_down", [4096, D])
    vecs_d = dram("vecs", [128, 1024 + 48])
    bglu_d = dram("bglu", [128, 2048])
    gfm_d = dram("gfm", [128, 32])
    convp_d = dram("convp", [128, 12 * 5])
    consts_d = dram("consts", [128, 5 * 128])
    s5p_d = dram("s5p", [128, 32 * 3])
    s5b_d = dram("s5b", [128, 32 * 2 * 16])
    s5c_d = dram("s5c", [128, 32 * 2 * 16])
    s5d_d = dram("s5d", [128, 8])
    out_d = dram("out", [128 * nchunks, D], kind="ExternalOutput")
    wscr = dram("wscr", [NBLK, 128, 8 * 512], BF16, kind="Internal")
    dbg_d = dram("dbg", [128, 8192], kind="ExternalOutput") if stop is not None else None

    with ExitStack() as es:
        P = Prog(nc, es)
        cnt = [0]

        def sb(shape, dt=F32, name=None):
            cnt[0] += 1
            nm = name or ("t%d" % cnt[0])
            return es.enter_context(nc.sbuf_tensor(nm, shape, dt)), Res(nm)

        def psum(shape, dt=F32):
            cnt[0] += 1
            nm = "ps%d" % cnt[0]
            return es.enter_context(nc.psum_tensor(nm, shape, dt)), Res(nm)

        V = lambda fn, r=(), w=(): P.op('dve', fn, r, w)
        A = lambda fn, r=(), w=(): P.op('act', fn, r, w)
        G = lambda fn, r=(), w=(): P.op('pool', fn, r, w)
        T = lambda fn, r=(), w=(): P.op('pe', fn, r, w)
        LD = lambda fn, r=(), w=(): P.op('sp', fn, r, w, dma=True)
        GD = lambda fn, r=(), w=(): P.op('pool', fn, r, w, dma=True)

        r_dbg = Res("dbg")

        def stage(name, dumps, ci=None):
            if stop != name or (ci is not None and ci != stop_chunk):
                return
            col = 0
            for ap, r, n in dumps:
                GD(lambda e, ap=ap, col=col, n=n: e.dma_start(out=dbg_d[:, col:col + n], in_=ap), [r], [r_dbg])
                col += n
            raise _Stop()

        pT, r_pT = psum([128, 1024], BF16)
        pm = [psum([128, 512]) for _ in range(2)]
        pss = [psum([128, 512]) for _ in range(2)]
        pq3 = [psum([128, 512]) for _ in range(3)]
        pf0 = pss[0][0][:].rearrange("p (a b) -> p a b", a=4)
        pf1 = pss[1][0][:].rearrange("p (a b) -> p a b", a=4)
        pfs = [pf0, pf1]
        r_pfs = [pss[0][1], pss[1][1]]
        consts, r_consts = sb([128, 640])
        LD(lambda e: e.dma_start(out=consts[:], in_=consts_d[:, :]), w=[r_consts])
        ident = consts[:, 0:128]
        tri = consts[:, 128:256]
        maskneg = consts[:, 256:384]
        jtab = consts[:, 384:512]
        mask112 = consts[:, 512:513]
        identb, r_identb = sb([128, 128], BF16)
        V(lambda e: e.tensor_copy(out=identb[:], in_=ident), [r_consts], [r_identb])
        ones, r_ones = sb([128, 128])
        V(lambda e: e.memset(ones[:], 1.0), [], [r_ones])
        onesb, r_onesb = sb([128, 128], BF16)
        V(lambda e: e.memset(onesb[:], 1.0), [], [r_onesb])
        epsc, _r_eps = sb([128, 1])
        V(lambda e: e.memset(epsc[:], 1e-5), [], [r_ones])

        vecs, r_vecs = sb([128, 1024 + 48])
        bglu, r_bglu = sb([128, 2048], BF16)
        GD(lambda e: e.dma_start(out=bglu[:], in_=bglu_d[:, :]), w=[r_bglu])
        gfm, r_gfm = sb([128, 32])
        LD(lambda e: e.dma_start(out=gfm[:], in_=gfm_d[:, :]), w=[r_gfm])
        LD(lambda e: e.dma_start(out=vecs[:], in_=vecs_d[:, :]), w=[r_vecs])
        g_mix, g_ssd, g_s5, g_mlp = 0, 1, 2, 3
        g_fin = vecs[:, 0:1024]
        b_glu = bglu
        dtb = vecs[:, 1024:1040]
        dssd = vecs[:, 1056:1072]
        arep, r_arep = sb([128, 16])
        A(lambda e: e.activation(out=arep[:], in_=vecs[:, 1040:1056], func=AF.Exp), [r_vecs], [r_arep])
        V(lambda e: e.tensor_scalar(out=arep[:], in0=arep[:], scalar1=-1.0, scalar2=None, op0=ALU.mult), [r_arep], [r_arep])

        convp, r_convp = sb([128, 60])
        LD(lambda e: e.dma_start(out=convp[:], in_=convp_d[:, :]), w=[r_convp])
        s5d, r_s5d = sb([128, 8])
        LD(lambda e: e.dma_start(out=s5d[:], in_=s5d_d[:, :]), w=[r_s5d])

        dgts = [sb([128, 128], BF16) for _ in range(2)]
        r_wscr = Res("wscr")
        aT, r_aT = sb([128, 32, 128], BF16)
        wstage = (aT[:].rearrange("p (a b) c -> p a (b c)", a=8), r_aT)

        def cast_block(blk, src, k0, c0, ncols=512, gidx=None):
            srcap = src[k0:k0 + 1024, c0:c0 + ncols].rearrange("(kc p) c -> p kc c", p=128)
            dst = wscr[blk].rearrange("p (kc c) -> p kc c", kc=8)[:, :, 0:ncols]
            stg, r_stg = wstage
            GD(lambda e: e.dma_start(out=stg[:, :, 0:ncols], in_=srcap), w=[r_stg])
            if gidx is not None:
                for kc in range(8):
                    V(lambda e, kc=kc: e.tensor_scalar(out=stg[:, kc, 0:ncols], in0=stg[:, kc, 0:ncols], scalar1=gfm[:, 8 * gidx + kc:8 * gidx + kc + 1], scalar2=None, op0=ALU.mult), [r_stg, r_gfm], [r_stg])
            LD(lambda e: e.dma_start(out=dst, in_=stg[:, :, 0:ncols]), [r_stg], [r_wscr])

        for b in range(NB_IN):
            cast_block(OFF_IN + b, w_in_d, 0, 512 * b, gidx=0)
        for b in range(NB_GLU):
            cast_block(OFF_GLU + b, w_glu_d, 0, 512 * b)
        for b in range(NB_OUT):
            cast_block(OFF_OUT + b, w_out_d, 1024 * (b // 2), 512 * (b % 2), gidx=1 + b // 2)
        for b in range(NB_UP):
            cast_block(OFF_UP + b, w_up_d, 0, 512 * b, gidx=3)
        for b in range(NB_DN):
            cast_block(OFF_DN + b, w_dn_d, 1024 * (b // 2), 512 * (b % 2))

        NWB = 4
        wbufs = [sb([128, 8, 512], BF16) for _ in range(NWB)]
        wb_i = [0]

        def load_w(blk):
            t, r = wbufs[wb_i[0] % NWB]
            wb_i[0] += 1
            LD(lambda e: e.dma_start(out=t[:].rearrange("p a c -> p (a c)"), in_=wscr[blk]), [r_wscr], [r])
            return t, r

        xh, r_xh = sb([128, 2, 4, 128])
        tt, r_tt = sb([128, 2, 4, 128])
        ww, r_ww = sb([128, 2, 4, 128])
        xtm, r_xtm = sb([128, 1024])
        yg, r_yg = sb([128, 1024])
        car, r_car = sb([128, 64])
        s5p, r_s5p = sb([128, 96])
        LD(lambda e: e.dma_start(out=s5p[:], in_=s5p_d[:, :]), w=[r_s5p])
        s5b, r_s5b = xh[:].rearrange("p a b c -> p (a b c)").rearrange("p (a b c) -> p a b c", a=32, b=2), r_xh
        LD(lambda e: e.dma_start(out=xh[:].rearrange("p a b c -> p (a b c)"), in_=s5b_d[:, :]), w=[r_s5b])
        s5c, r_s5c = tt[:].rearrange("p a b c -> p (a b c)").rearrange("p (a b c) -> p a b c", a=32, b=2), r_tt
        LD(lambda e: e.dma_start(out=tt[:].rearrange("p a b c -> p (a b c)"), in_=s5c_d[:, :]), w=[r_s5c])
        lr = s5p[:, 0:32]
        li = s5p[:, 32:64]
        sp_, r_sp = sb([128, 16, 32])
        pl = lambda i: sp_[:, i, :]
        R1 = [r_s5p, r_sp]
        A(lambda e: e.activation(out=pl(0), in_=s5p[:, 64:96], func=AF.Exp), R1, [r_sp])
        V(lambda e: e.tensor_tensor(out=pl(1), in0=lr, in1=pl(0), op=ALU.mult), R1, [r_sp])
        A(lambda e: e.activation(out=pl(2), in_=pl(1), func=AF.Exp), R1, [r_sp])
        V(lambda e: e.tensor_tensor(out=pl(3), in0=li, in1=pl(0), op=ALU.mult), R1, [r_sp])
        V(lambda e: e.tensor_scalar(out=pl(3), in0=pl(3), scalar1=1.0 / (2 * np.pi), scalar2=None, op0=ALU.mult), R1, [r_sp])

        def sincos(dst_s, dst_c, turns, rr, ww, tmp1, tmp2):
            V(lambda e: e.tensor_scalar(out=tmp1, in0=turns, scalar1=MAGIC, scalar2=MAGIC, op0=ALU.add, op1=ALU.subtract), rr, ww)
            V(lambda e: e.tensor_tensor(out=tmp1, in0=turns, in1=tmp1, op=ALU.subtract), rr, ww)
            V(lambda e: e.tensor_scalar(out=dst_s, in0=tmp1, scalar1=-0.25, scalar2=0.25, op0=ALU.max, op1=ALU.min), rr, ww)
            V(lambda e: e.scalar_tensor_tensor(out=tmp1, in0=dst_s, scalar=2.0, in1=tmp1, op0=ALU.mult, op1=ALU.subtract), rr, ww)
            A(lambda e: e.activation(out=dst_s, in_=tmp1, func=AF.Sin, scale=TWO_PI), rr, ww)
            V(lambda e: e.tensor_scalar(out=tmp2, in0=turns, scalar1=0.25, scalar2=None, op0=ALU.add), rr, ww)
            V(lambda e: e.tensor_scalar(out=tmp1, in0=tmp2, scalar1=MAGIC, scalar2=MAGIC, op0=ALU.add, op1=ALU.subtract), rr, ww)
            V(lambda e: e.tensor_tensor(out=tmp1, in0=tmp2, in1=tmp1, op=ALU.subtract), rr, ww)
            V(lambda e: e.tensor_scalar(out=dst_c, in0=tmp1, scalar1=-0.25, scalar2=0.25, op0=ALU.max, op1=ALU.min), rr, ww)
            V(lambda e: e.scalar_tensor_tensor(out=tmp1, in0=dst_c, scalar=2.0, in1=tmp1, op0=ALU.mult, op1=ALU.subtract), rr, ww)
            A(lambda e: e.activation(out=dst_c, in_=tmp1, func=AF.Sin, scale=TWO_PI), rr, ww)

        sincos(pl(4), pl(5), pl(3), R1, [r_sp], pl(14), pl(15))
        V(lambda e: e.tensor_tensor(out=pl(6), in0=pl(2), in1=pl(5), op=ALU.mult), R1, [r_sp])
        V(lambda e: e.tensor_tensor(out=pl(7), in0=pl(2), in1=pl(4), op=ALU.mult), R1, [r_sp])
        V(lambda e: e.tensor_scalar(out=pl(6), in0=pl(6), scalar1=-1.0, scalar2=None, op0=ALU.add), R1, [r_sp])
        V(lambda e: e.tensor_tensor(out=pl(8), in0=lr, in1=lr, op=ALU.mult), R1, [r_sp])
        V(lambda e: e.tensor_tensor(out=pl(9), in0=li, in1=li, op=ALU.mult), R1, [r_sp])
        V(lambda e: e.tensor_tensor(out=pl(8), in0=pl(8), in1=pl(9), op=ALU.add), R1, [r_sp])
        V(lambda e: e.reciprocal(out=pl(8), in_=pl(8)), R1, [r_sp])
        V(lambda e: e.tensor_tensor(out=pl(9), in0=pl(6), in1=lr, op=ALU.mult), R1, [r_sp])
        V(lambda e: e.tensor_tensor(out=pl(10), in0=pl(7), in1=li, op=ALU.mult), R1, [r_sp])
        V(lambda e: e.tensor_tensor(out=pl(9), in0=pl(9), in1=pl(10), op=ALU.add), R1, [r_sp])
        V(lambda e: e.tensor_tensor(out=pl(9), in0=pl(9), in1=pl(8), op=ALU.mult), R1, [r_sp])
        V(lambda e: e.tensor_tensor(out=pl(10), in0=pl(7), in1=lr, op=ALU.mult), R1, [r_sp])
        V(lambda e: e.tensor_tensor(out=pl(11), in0=pl(6), in1=li, op=ALU.mult), R1, [r_sp])
        V(lambda e: e.tensor_tensor(out=pl(10), in0=pl(10), in1=pl(11), op=ALU.subtract), R1, [r_sp])
        V(lambda e: e.tensor_tensor(out=pl(10), in0=pl(10), in1=pl(8), op=ALU.mult), R1, [r_sp])
        V(lambda e: e.tensor_scalar(out=pl(11), in0=pl(3), scalar1=128.0, scalar2=None, op0=ALU.mult), R1, [r_sp])
        sincos(pl(12), pl(13), pl(11), R1, [r_sp], pl(14), pl(15))
        rho = pl(2)
        Stab = pl(12)
        Ctab = pl(13)
        bb, r_bb = ww[:].rearrange("p a b c -> p (a b c)").rearrange("p (a b c) -> p a b c", a=2, b=32), r_ww
        tmpb, r_tmpb = xtm[:, 0:512].rearrange("p (a b) -> p a b", a=32), r_xtm
        cre_b = pl(9).unsqueeze(2).to_broadcast([128, 32, 16])
        cim_b = pl(10).unsqueeze(2).to_broadcast([128, 32, 16])
        RB = [r_sp, r_s5b, r_bb, r_tmpb]
        V(lambda e: e.tensor_tensor(out=bb[:, 0], in0=s5b[:, :, 0, :], in1=cre_b, op=ALU.mult), RB, [r_bb])
        V(lambda e: e.tensor_tensor(out=tmpb, in0=s5b[:, :, 1, :], in1=cim_b, op=ALU.mult), RB, [r_tmpb])
        V(lambda e: e.tensor_tensor(out=bb[:, 0], in0=bb[:, 0], in1=tmpb, op=ALU.subtract), RB, [r_bb])
        V(lambda e: e.tensor_tensor(out=bb[:, 1], in0=s5b[:, :, 1, :], in1=cre_b, op=ALU.mult), RB, [r_bb])
        V(lambda e: e.tensor_tensor(out=tmpb, in0=s5b[:, :, 0, :], in1=cim_b, op=ALU.mult), RB, [r_tmpb])
        V(lambda e: e.tensor_tensor(out=bb[:, 1], in0=bb[:, 1], in1=tmpb, op=ALU.add), RB, [r_bb])
        Bw, r_Bw = sb([128, 2, 32, 128], BF16)
        raccs = [sb([128, 512], BF16) for _ in range(2)]
        mpad, r_mpad = sb([128, 128])
        for ri in range(2):
            for pair in range(32):
                hj = pair % 4
                G(lambda e: e.memset(mpad[:], 0.0), [], [r_mpad])
                for g2 in range(2):
                    c0 = hj * 32 + g2 * 16
                    G(lambda e, g2=g2, c0=c0, ri=ri, pair=pair: e.tensor_copy(out=mpad[64 * g2:64 * g2 + 64, c0:c0 + 16], in_=bb[64 * g2:64 * g2 + 64, ri, pair, :]), [r_bb], [r_mpad])
                T(lambda e: e.transpose(pss[0][0][:, 0:128], mpad[:], ident), [r_mpad, r_consts], [pss[0][1]])
                V(lambda e, ri=ri, pair=pair: e.tensor_copy(out=Bw[:, ri, pair, :], in_=pss[0][0][:, 0:128]), [pss[0][1]], [r_Bw])
        Cw, r_Cw = sb([128, 32, 3, 64], BF16)
        V(lambda e: e.memset(Cw[:].rearrange("p a b c -> p (a b c)"), 0.0), [], [r_Cw])
        for pair in range(32):
            j = pair % 2
            for g2 in range(2):
                c0 = j * 32 + g2 * 16
                ps_ = slice(64 * g2, 64 * g2 + 64)
                V(lambda e, pair=pair, c0=c0, ps_=ps_: e.tensor_copy(out=Cw[ps_, pair, 0, c0:c0 + 16], in_=s5c[ps_, pair, 0, :]), [r_s5c], [r_Cw])
                V(lambda e, pair=pair, c0=c0, ps_=ps_: e.tensor_scalar(out=Cw[ps_, pair, 1, c0:c0 + 16], in0=s5c[ps_, pair, 0, :], scalar1=-1.0, scalar2=None, op0=ALU.mult), [r_s5c], [r_Cw])
                V(lambda e, pair=pair, c0=c0, ps_=ps_: e.tensor_scalar(out=Cw[ps_, pair, 2, c0:c0 + 16], in0=s5c[ps_, pair, 1, :], scalar1=-1.0, scalar2=None, op0=ALU.mult), [r_s5c], [r_Cw])
        cosT, r_cosT = sb([128, 32, 128])
        sinT, r_sinT = sb([128, 32, 128])
        RT = [r_xtm, r_yg, r_cosT, r_sinT]
        for q in range(4):
            for pp_ in range(8):
                pair = 8 * q + pp_
                V(lambda e, pair=pair, pp_=pp_: e.tensor_scalar(out=xtm[:, 128 * pp_:128 * (pp_ + 1)], in0=jtab, scalar1=sp_[:, 3, pair:pair + 1], scalar2=None, op0=ALU.mult), [r_consts, r_sp], [r_xtm])
            fl = lambda t, q=q: t[:, 8 * q:8 * q + 8, :].rearrange("p a b -> p (a b)")
            sincos(fl(sinT), fl(cosT), xtm[:, :], RT, RT, yg[:, :], fl(cosT))

        stopped = [False]
        try:
            stage('init', [(sp_[:].rearrange("p a b -> p (a b)"), r_sp, 512), (cosT[:, 0:4, :].rearrange("p a b -> p (a b)"), r_cosT, 512), (sinT[:, 0:4, :].rearrange("p a b -> p (a b)"), r_sinT, 512),
                           (Bw[:, 0, 0:4, :].rearrange("p a b -> p (a b)"), r_Bw, 512), (Bw[:, 1, 0:4, :].rearrange("p a b -> p (a b)"), r_Bw, 512), (Cw[:, 0:4, :, :].rearrange("p a b c -> p (a b c)"), r_Cw, 768)])
        except _Stop:
            stopped[0] = True
        Sst, r_S = sb([128, 1, 1024])
        V(lambda e: e.memset(Sst[:].rearrange("p a b -> p (a b)"), 0.0), [], [r_S])
        Sb, r_Sb = sb([128, 1024], BF16)
        V(lambda e: e.memset(Sb[:], 0.0), [], [r_Sb])
        winit, r_winit = sb([128, 2, 32])
        V(lambda e: e.memset(winit[:].rearrange("p a b -> p (a b)"), 0.0), [], [r_winit])
        zl, r_zl = sb([128, 2, 32])
        xbc_raw, r_xr = sb([128, 12, 131])
        V(lambda e: e.memset(xbc_raw[:].rearrange("p a b -> p (a b)"), 0.0), [], [r_xr])

        xt2 = [sb([128, 1024]) for _ in range(2)]
        xn, r_xn = sb([128, 1024], BF16)
        st, r_st = sb([128, 16])
        nT, r_nT = sb([128, 8, 128], BF16)
        xact, r_xact = sb([128, 8, 128])
        bc, r_bc = sb([128, 4, 128], BF16)
        cacc, r_cacc = sb([128, 128])
        cacc2, r_cacc2 = sb([128, 128])
        caccs = [(cacc, r_cacc), (cacc2, r_cacc2)]
        ubs = [sb([128, 8, 128], BF16) for _ in range(2)]
        zs, r_zs = sb([128, 1024], BF16)
        dts, r_dts = sb([128, 8, 16])
        xdt, r_xdt = sb([128, 1024], BF16)
        xds, r_xds = sb([128, 1024], BF16)
        btm, r_btm = sb([128, 256], BF16)
        eargs = [sb([128, 4, 128]) for _ in range(2)]
        ehms = [sb([128, 4, 128], BF16) for _ in range(2)]
        eareps = [sb([128, 4, 128], BF16) for _ in range(2)]
        yg2, r_yg2 = sb([128, 1024])
        ygs = [(yg, r_yg), (yg2, r_yg2)]
        hnT, r_hnT = nT, r_nT
        pr, r_pr = sb([128, 4, 4, 128], BF16)
        gels = [sb([128, 8, 128], BF16) for _ in range(2)]
        mixT, r_mixT = sb([128, 16, 128], BF16)
        hh, r_hh = sb([128, 1024])


        def rstd_of(src_ap, L, col, rsrc):
            A(lambda e: e.activation(out=xn[:L, :], in_=src_ap, func=AF.Square, accum_out=st[:L, col:col + 1]), rsrc, [r_xn, r_st])
            A(lambda e: e.activation(out=st[:L, col:col + 1], in_=st[:L, col:col + 1], func=AF.Sqrt, scale=1.0 / 1024, bias=epsc[:L, 0:1]), [r_st, r_ones], [r_st])
            V(lambda e: e.reciprocal(out=st[:L, col:col + 1], in_=st[:L, col:col + 1]), [r_st], [r_st])

        def norm_T(src_ap, rsrc, gain, L, col, dstT, kc0):
            rstd_of(src_ap, L, col, rsrc)
            A(lambda e: e.activation(out=xn[:L, :], in_=src_ap, func=AF.Identity, scale=st[:L, col:col + 1]), rsrc + [r_st], [r_xn])
            for k in range(8):
                T(lambda e, k=k: e.transpose(pT[:, 128 * k:128 * k + L], xn[:L, 128 * k:128 * (k + 1)], identb[:L, :L]), [r_xn, r_identb], [r_pT])
            A(lambda e: e.activation(out=dstT[:, kc0:kc0 + 8, :L], in_=pT[:].rearrange("p (a b) -> p a b", a=8)[:, :, :L], func=AF.Identity), [r_pT], [dstT_res[id(dstT)]])

        r_tt2 = [[Res('tt'), Res('tt')] for _ in range(2)]; r_xh2 = [[Res('xh'), Res('xh')] for _ in range(2)]; r_ww2 = [[Res('ww') for _ in range(4)] for _ in range(2)]; r_pr2 = [Res('pr0'), Res('pr1')]
        dstT_res = {id(nT): r_nT, id(mixT): r_mixT}
        r_out = Res("out")

        chunks = [(128 * i, 128) for i in range(nchunks + 1)]
        L = 128
        bc3 = lambda ap, shape, axis: ap.unsqueeze(axis).to_broadcast(shape)

        q3 = []
        deferred = []
        deferred_next = []
        pendingE = set()
        emittedE = set()
        slots_left = [20]

        eq = []

        def flushE():
            for key, f in eq:
                f()
                emittedE.add(key)
                pendingE.discard(key)
            eq[:] = []

        def pump():
            flushE()
            n = min(4, max(1, -(-len(q3) // max(slots_left[0], 1))))
            slots_left[0] = max(slots_left[0] - 1, 1)
            k = 0
            while k < n and q3:
                pc = q3[0]
                if pc['barrier'] and pendingE:
                    break
                if pc['needs'] is not None and pc['needs'] not in emittedE:
                    break
                q3.pop(0)
                pc['M']()
                k += 1
                if pc['E'] is not None:
                    eq.append((pc['key'], pc['E']))
                    pendingE.add(pc['key'])

        def flush():
            for key, f in deferred:
                f()
                if key is not None:
                    emittedE.add(key)
                    pendingE.discard(key)
            deferred[:] = deferred_next
            deferred_next[:] = []

        def dt_chain(pq, rq):
            V(lambda e: e.tensor_copy(out=dts[:L, 2, :], in_=pq[:L, 0:16]), [rq], [r_dts])
            V(lambda e: e.tensor_scalar(out=dts[:L, 3, :], in0=pq[:L, 0:16], scalar1=-1.0, scalar2=None, op0=ALU.mult), [rq], [r_dts])
            V(lambda e: e.tensor_copy(out=dts[:, 4, :], in_=pq[:, 16:32]), [rq], [r_dts])
            V(lambda e: e.tensor_tensor(out=dts[:L, 7, :], in0=dts[:L, 4, :], in1=dts[:L, 2, :], op=ALU.subtract), [r_dts], [r_dts])
            A(lambda e: e.activation(out=dts[:L, 5, :], in_=dts[:L, 7, :], func=AF.Exp), [r_dts], [r_dts])
            A(lambda e: e.activation(out=dts[:, 6, :], in_=dts[:, 4, :], func=AF.Exp), [r_dts], [r_dts])
            V(lambda e: e.tensor_tensor(out=dts[:L, 7, :], in0=dts[:L, 0, :], in1=dts[:L, 5, :], op=ALU.mult), [r_dts], [r_dts])

        def make_inproj(cj):
            t0 = chunks[cj][0]
            xt, r_xt = xt2[cj % 2]
            ub, r_ub = ubs[cj % 2]
            out = []

            def pcj(name, M, E=None, barrier=False, needs=None):
                out.append({'key': (cj, name), 'M': M, 'E': E, 'barrier': barrier, 'needs': (cj, needs) if needs is not None else None})

            def ipnorm():
                if cj + 1 < len(chunks):
                    xtn, r_xtn = xt2[(cj + 1) % 2]
                    tn = chunks[cj + 1][0]
                    LD(lambda e: e.dma_start(out=xtn[:, :], in_=x_d[tn:tn + L, :]), w=[r_xtn])
                norm_T(xt[:, :], [r_xt], g_mix, L, 0, nT, 0)
            pcj('ipnorm', ipnorm, barrier=True)
            wts = {}

            def fm_M(blk):
                wt, rw = load_w(OFF_IN + blk)
                pp, rp = pq3[blk % 3]
                for m in range(4):
                    for kc in range(8):
                        T(lambda e, m=m, kc=kc: e.matmul(pp[:, 128 * m:128 * (m + 1)], lhsT=wt[:, kc, 128 * m:128 * (m + 1)], rhs=nT[:, kc, :L], start=(kc == 0), stop=(kc == 7)), [rw, r_nT], [rp])

            def fm_E(blk):
                pp, rp = pq3[blk % 3]
                p3v = pp[:, :].rearrange("p (a b) -> p a b", a=4)
                if blk < 3:
                    A(lambda e: e.activation(out=xbc_raw[:, 4 * blk:4 * blk + 4, 3:3 + L], in_=p3v, func=AF.Identity), [rp], [r_xr])
                else:
                    A(lambda e: e.activation(out=ub[:, 4 * (blk - 3):4 * (blk - 3) + 4, :L], in_=p3v, func=AF.Identity), [rp], [r_ub])
            for blk in range(5):
                pcj('ipb%d' % blk, lambda blk=blk: fm_M(blk), lambda blk=blk: fm_E(blk), needs=('ipb%d' % (blk - 3) if blk >= 3 else None))

            def tm_M(blk):
                idx = 5 + blk
                wt, rw = load_w(OFF_IN + 5 + blk)
                pp, rp = pq3[idx % 3]
                ncol = 512 if blk < 2 else 16
                for kc in range(8):
                    T(lambda e, kc=kc: e.matmul(pp[:L, :ncol], lhsT=nT[:, kc, :L], rhs=wt[:, kc, :ncol], start=(kc == 0), stop=(kc == 7)), [rw, r_nT], [rp])

            def tm_E(blk):
                idx = 5 + blk
                pp, rp = pq3[idx % 3]
                if blk < 2:
                    A(lambda e: e.activation(out=zs[:L, 512 * blk:512 * (blk + 1)], in_=pp[:L, :], func=AF.Silu), [rp], [r_zs])
                else:
                    V(lambda e: e.tensor_tensor(out=dts[:L, 7, :], in0=pp[:L, :16], in1=dtb[:L, :], op=ALU.add), [rp, r_vecs], [r_dts])
                    A(lambda e: e.activation(out=dts[:L, 7, :], in_=dts[:L, 7, :], func=AF.Exp), [r_dts], [r_dts])
                    A(lambda e: e.activation(out=dts[:L, 0, :], in_=dts[:L, 7, :], func=AF.Ln, bias=1.0), [r_dts], [r_dts])
                    V(lambda e: e.tensor_tensor(out=dts[:L, 1, :], in0=dts[:L, 0, :], in1=arep[:L, :], op=ALU.mult), [r_dts, r_arep], [r_dts])
            for blk in range(3):
                pcj('ipt%d' % blk, lambda blk=blk: tm_M(blk), lambda blk=blk: tm_E(blk), needs='ipb%d' % (2 + blk))

            def dt_M():
                pq, rq = pq3[2]
                T(lambda e: e.matmul(pq[:L, 0:16], lhsT=tri[:L, :L], rhs=dts[:L, 1, :], start=True, stop=True), [r_consts, r_dts], [rq])
                T(lambda e: e.matmul(pq[:, 16:32], lhsT=ones[:L, :], rhs=dts[:L, 1, :], start=True, stop=True), [r_ones, r_dts], [rq])

            def dt_E():
                pq, rq = pq3[2]
                dt_chain(pq, rq)
            pcj('ipdt', dt_M, dt_E, needs='ipt2')
            return out

        def make_chunk(ci, t0):
            real = ci > 0
            xt, r_xt = xt2[ci % 2]
            yg, r_yg = ygs[ci % 2]
            gel, r_gel = gels[ci % 2]
            ub, r_ub = ubs[ci % 2]
            p1 = []
            p3 = []

            def in_proj():
                if ci == 0:
                    LD(lambda e: e.dma_start(out=xt[:, :], in_=x_d[t0:t0 + L, :]), w=[r_xt])
                if ci + 1 < len(chunks):
                    xtn, r_xtn = xt2[(ci + 1) % 2]
                    tn = chunks[ci + 1][0]
                    LD(lambda e: e.dma_start(out=xtn[:, :], in_=x_d[tn:tn + L, :]), w=[r_xtn])
                norm_T(xt[:, :], [r_xt], g_mix, L, 0, nT, 0)
                stage('norm', [(nT[:].rearrange("p a b -> p (a b)"), r_nT, 1024), (st[:, :], r_st, 16), (xt[:, :], r_xt, 1024)], ci)
                for blk in range(5):
                    wt, rw = load_w(OFF_IN + blk)
                    for m in range(4):
                        mt = blk * 4 + m
                        pp, rp = pm[mt % 2]
                        for kc in range(8):
                            T(lambda e, pp=pp, wt=wt, m=m, kc=kc: e.matmul(pp[:, :L], lhsT=wt[:, kc, 128 * m:128 * (m + 1)], rhs=nT[:, kc, :L], start=(kc == 0), stop=(kc == 7)), [rw, r_nT], [rp])
                        if mt < 12:
                            V(lambda e, pp=pp, mt=mt: e.tensor_copy(out=xbc_raw[:, mt, 3:3 + L], in_=pp[:, :L]), [rp], [r_xr])
                        else:
                            V(lambda e, pp=pp, mt=mt: e.tensor_copy(out=ub[:, mt - 12, :L], in_=pp[:, :L]), [rp], [r_ub])
                for blk in range(3):
                    wt, rw = load_w(OFF_IN + 5 + blk)
                    pp, rp = pm[blk % 2]
                    ncol = 512 if blk < 2 else 16
                    for kc in range(8):
                        T(lambda e, pp=pp, wt=wt, kc=kc, ncol=ncol: e.matmul(pp[:L, :ncol], lhsT=nT[:, kc, :L], rhs=wt[:, kc, :ncol], start=(kc == 0), stop=(kc == 7)), [rw, r_nT], [rp])
                    if blk < 2:
                        A(lambda e, pp=pp, blk=blk: e.activation(out=zs[:L, 512 * blk:512 * (blk + 1)], in_=pp[:L, :], func=AF.Silu), [rp], [r_zs])
                    else:
                        V(lambda e, pp=pp: e.tensor_tensor(out=dts[:L, 7, :], in0=pp[:L, :16], in1=dtb[:L, :], op=ALU.add), [rp, r_vecs], [r_dts])
                        A(lambda e: e.activation(out=dts[:L, 7, :], in_=dts[:L, 7, :], func=AF.Exp), [r_dts], [r_dts])
                        A(lambda e: e.activation(out=dts[:L, 0, :], in_=dts[:L, 7, :], func=AF.Ln, bias=1.0), [r_dts], [r_dts])
                        if not real:
                            V(lambda e: e.tensor_scalar(out=dts[:, 0, :], in0=dts[:, 0, :], scalar1=mask112, scalar2=None, op0=ALU.mult), [r_dts, r_consts], [r_dts])
                        V(lambda e: e.tensor_tensor(out=dts[:L, 1, :], in0=dts[:L, 0, :], in1=arep[:L, :], op=ALU.mult), [r_dts, r_arep], [r_dts])
                stage('proj', [(nT[:].rearrange("p a b -> p (a b)"), r_nT, 1024), (xbc_raw[:, :, 3:131], r_xr, 1536), (ub[:].rearrange("p a b -> p (a b)"), r_ub, 1024), (zs[:, :], r_zs, 1024), (dts[:].rearrange("p a b -> p (a b)"), r_dts, 128), (st[:, :], r_st, 16)], ci)
            if ci == 0:
                p1.append(in_proj)

            def conv():
                for mt in range(12):
                    ca, r_ca = caccs[mt % 2]
                    cw = lambda k, mt=mt: convp[:, 5 * mt + k:5 * mt + k + 1]
                    V(lambda e, mt=mt, cw=cw, ca=ca: e.tensor_scalar(out=ca[:, :L], in0=xbc_raw[:, mt, 0:L], scalar1=cw(0), scalar2=None, op0=ALU.mult), [r_xr, r_convp], [r_ca])
                    for k in range(1, 4):
                        V(lambda e, mt=mt, cw=cw, k=k, ca=ca: e.scalar_tensor_tensor(out=ca[:, :L], in0=xbc_raw[:, mt, k:k + L], scalar=cw(k), in1=ca[:, :L], op0=ALU.mult, op1=ALU.add), [r_xr, r_convp, r_ca], [r_ca])
                    if mt < 8:
                        A(lambda e, mt=mt, cw=cw, ca=ca: e.activation(out=xact[:, mt, :L], in_=ca[:, :L], func=AF.Silu, bias=cw(4)), [r_ca, r_convp], [r_xact])
                    else:
                        A(lambda e, mt=mt, cw=cw, ca=ca: e.activation(out=bc[:, mt - 8, :L], in_=ca[:, :L], func=AF.Silu, bias=cw(4)), [r_ca, r_convp], [r_bc])
                for mt in range(12):
                    G(lambda e, mt=mt: e.tensor_copy(out=xbc_raw[:, mt, 0:3], in_=xbc_raw[:, mt, L:L + 3]), [r_xr], [r_xr])
                stage('conv', [(xact[:].rearrange("p a b -> p (a b)"), r_xact, 1024), (bc[:].rearrange("p a b -> p (a b)"), r_bc, 512)], ci)
            p1.append(lambda: (pump(), conv(), flush()))

            def ssd_pre():
                pp, rp = pss[0]
                for k in range(8):
                    T(lambda e, pp=pp, k=k: e.transpose(pp[:L, 128 * (k % 4):128 * (k % 4 + 1)], xact[:, k, :L], ident), [r_xact, r_consts], [rp])
                    if k % 4 == 3:
                        A(lambda e, pp=pp, k=k: e.activation(out=xtm[:L, 512 * (k // 4):512 * (k // 4 + 1)], in_=pp[:L, :], func=AF.Identity), [rp], [r_xtm])
                for g in range(2):
                    T(lambda e, g=g: e.transpose(pT[:L, 128 * g:128 * (g + 1)], bc[:, g, :L], identb[:, :]), [r_bc, r_identb], [r_pT])
                A(lambda e: e.activation(out=btm[:L, :], in_=pT[:L, 0:256], func=AF.Identity), [r_pT], [r_btm])
                if ci == 0:
                    pq, rq = pss[1]
                    T(lambda e: e.matmul(pq[:L, 0:16], lhsT=tri[:L, :L], rhs=dts[:L, 1, :], start=True, stop=True), [r_consts, r_dts], [rq])
                    T(lambda e: e.matmul(pq[:, 16:32], lhsT=ones[:L, :], rhs=dts[:L, 1, :], start=True, stop=True), [r_ones, r_dts], [rq])
                    dt_chain(pq, rq)
                x3 = xtm[:, :].rearrange("p (h c) -> p h c", h=16)
                V(lambda e: e.tensor_tensor(out=xdt[:, :].rearrange("p (h c) -> p h c", h=16), in0=x3, in1=bc3(dts[:, 0, :], [128, 16, 64], 2), op=ALU.mult), [r_xtm, r_dts], [r_xdt])
                V(lambda e: e.tensor_tensor(out=xds[:, :].rearrange("p (h c) -> p h c", h=16), in0=x3, in1=bc3(dts[:, 7, :], [128, 16, 64], 2), op=ALU.mult), [r_xtm, r_dts], [r_xds])
            p1.append(lambda: (ssd_pre(), flush()))

            def ssd_arow(g):
                for q4 in range(2):
                    h0 = 8 * g + 4 * q4
                    pa, ra = pss[q4]
                    pa3 = pa[:].rearrange("p (a b) -> p a b", a=4)
                    for i4 in range(4):
                        h = h0 + i4
                        T(lambda e, i4=i4, h=h, pa3=pa3: e.matmul(pa3[:, i4, :], lhsT=dts[:L, 1, h:h + 1].to_broadcast([L, 128]), rhs=tri[:L, :L], start=True, stop=True), [r_dts, r_consts], [ra])

            def ssd_group(g):
                py, ry = pm[g]
                po, ro = pm[1 - g]
                pcb = po[:, 0:128]
                gs = slice(512 * g, 512 * (g + 1))
                T(lambda e: e.matmul(pcb, lhsT=bc[:, g, :L], rhs=bc[:, 2 + g, :L], start=True, stop=True), [r_bc], [ro])
                if g == 0:
                    ssd_arow(0)
                pump()
                for q4 in range(2):
                    h0 = 8 * g + 4 * q4
                    pa, ra = pss[q4]
                    pa3 = pa[:].rearrange("p (a b) -> p a b", a=4)
                    earg, r_earg = eargs[q4]
                    V(lambda e, earg=earg, pa3=pa3: e.tensor_tensor(out=earg[:], in0=pa3, in1=bc3(maskneg, [128, 4, 128], 1), op=ALU.add), [ra, r_consts], [r_earg])
                    V(lambda e, earg=earg, h0=h0: e.tensor_tensor(out=earg[:], in0=earg[:], in1=bc3(dts[:, 3, h0:h0 + 4], [128, 4, 128], 2), op=ALU.add), [r_earg, r_dts], [r_earg])
                for q4 in range(2):
                    pa, ra = pss[q4]
                    pa3 = pa[:].rearrange("p (a b) -> p a b", a=4)
                    earg, r_earg = eargs[q4]
                    ehm, r_ehm = ehms[q4]
                    earep, r_earep = eareps[q4]
                    A(lambda e, ehm=ehm, earg=earg: e.activation(out=ehm[:], in_=earg[:], func=AF.Exp), [r_earg], [r_ehm])
                    A(lambda e, earep=earep, pa3=pa3: e.activation(out=earep[:], in_=pa3, func=AF.Exp), [ra], [r_earep])
                for q4 in range(2):
                    ehm, r_ehm = ehms[q4]
                    earep, r_earep = eareps[q4]
                    V(lambda e, ehm=ehm: e.tensor_tensor(out=ehm[:], in0=ehm[:], in1=bc3(pcb, [128, 4, 128], 1), op=ALU.mult), [ro, r_ehm], [r_ehm])
                    G(lambda e, earep=earep: e.tensor_tensor(out=earep[:], in0=earep[:], in1=bc3(bc[:, 2 + g, :], [128, 4, 128], 1), op=ALU.mult), [r_bc, r_earep], [r_earep])
                for q4 in range(2):
                    h0 = 8 * g + 4 * q4
                    ehm, r_ehm = ehms[q4]
                    earep, r_earep = eareps[q4]
                    for i4 in range(4):
                        h = h0 + i4
                        hl = 4 * q4 + i4
                        hs = slice(64 * h, 64 * (h + 1))
                        T(lambda e, ehm=ehm, i4=i4, hl=hl, hs=hs: e.matmul(py[:L, 64 * hl:64 * (hl + 1)], lhsT=ehm[:, i4, :], rhs=xdt[:L, hs], start=True, stop=False), [r_ehm, r_xdt], [ry])
                        T(lambda e, earep=earep, i4=i4, hl=hl, hs=hs: e.matmul(py[:L, 64 * hl:64 * (hl + 1)], lhsT=earep[:, i4, :], rhs=Sb[:, hs], start=False, stop=True), [r_earep, r_Sb], [ry])
                if g == 0:
                    ssd_arow(1)
                g3 = lambda t: t[:, gs].rearrange("p (h c) -> p h c", h=8)
                V(lambda e: e.tensor_tensor(out=g3(yg), in0=g3(xtm), in1=bc3(dssd[:, 8 * g:8 * g + 8], [128, 8, 64], 2), op=ALU.mult), [r_xtm, r_vecs], [r_yg])
                V(lambda e: e.tensor_tensor(out=yg[:, gs], in0=yg[:, gs], in1=py[:, :], op=ALU.add), [r_yg, ry], [r_yg])
                V(lambda e: e.tensor_tensor(out=yg[:, gs], in0=yg[:, gs], in1=zs[:, gs], op=ALU.mult), [r_yg, r_zs], [r_yg])
                T(lambda e: e.matmul(po[:, :], lhsT=btm[:L, 128 * g:128 * (g + 1)], rhs=xds[:L, gs], start=True, stop=True), [r_btm, r_xds], [ro])
                s3 = Sst[:, 0, gs].rearrange("p (h c) -> p h c", h=8)
                V(lambda e: e.tensor_tensor(out=s3, in0=s3, in1=bc3(dts[:, 6, 8 * g:8 * g + 8], [128, 8, 64], 2), op=ALU.mult), [r_S, r_dts], [r_S])
                V(lambda e: e.tensor_tensor(out=Sst[:, 0, gs], in0=Sst[:, 0, gs], in1=po[:, :], op=ALU.add), [r_S, ro], [r_S])
                if g == 1:
                    V(lambda e: e.tensor_copy(out=Sb[:, :], in_=Sst[:, 0, :]), [r_S], [r_Sb])
                    stage('ssd', [(yg[:, :], r_yg, 1024), (xtm[:, :], r_xtm, 1024), (dts[:].rearrange("p a b -> p (a b)"), r_dts, 128), (Sst[:, 0, :], r_S, 1024)], ci)
            p1.append(lambda: (ssd_group(0), flush()))
            p1.append(lambda: (ssd_group(1), flush()))

            first = (ci == 0)

            def s5_view(u):
                t, hh2 = divmod(u, 2)
                b = u % 2
                pz, rz = pss[b]
                pz4 = pz[:].rearrange("p (r j b) -> p r j b", r=2, j=2)
                p0 = 4 * t + 2 * hh2
                return t, hh2, b, pz4, rz, p0, slice(2 * b, 2 * b + 2)

            def s5_A(u):
                t, hh2, b, pz4, rz, p0, js = s5_view(u)
                for j in range(2):
                    for ri in range(2):
                        T(lambda e, j=j, ri=ri: e.matmul(pz4[:, ri, j, :], lhsT=Bw[:, ri, p0 + j, :], rhs=ub[:, t, :], start=True, stop=True), [r_Bw, r_ub], [rz])

            def s5_mults(u):
                t, hh2, b, pz4, rz, p0, js = s5_view(u)
                cs = cosT[:, p0:p0 + 2, :]
                sn = sinT[:, p0:p0 + 2, :]
                old = [r_tt, r_xh, r_ww, r_pr] if first and u < 2 else []
                r_t, r_x = r_tt2[b], r_xh2[b]
                RR = [rz, r_cosT, r_sinT]
                V(lambda e: e.tensor_tensor(out=xh[:, 0:2, js, :], in0=pz4[:, 0:2], in1=bc3(cs, [128, 2, 2, 128], 1), op=ALU.mult), RR, [r_x[0], r_x[1]] + old)
                V(lambda e: e.tensor_tensor(out=tt[:, 0:2, js, :], in0=pz4[:, 0:2], in1=bc3(sn, [128, 2, 2, 128], 1), op=ALU.mult), RR, [r_t[0], r_t[1]])
                G(lambda e: e.tensor_tensor(out=xh[:, 0, js, :], in0=xh[:, 0, js, :], in1=tt[:, 1, js, :], op=ALU.add), [r_x[0], r_t[1]], [r_x[0]])
                G(lambda e: e.tensor_tensor(out=xh[:, 1, js, :], in0=xh[:, 1, js, :], in1=tt[:, 0, js, :], op=ALU.subtract), [r_x[1], r_t[0]], [r_x[1]])

            def s5_unit(u):
                if u == 0:
                    s5_A(0)
                    s5_A(1)
                    s5_mults(0)
                if u + 2 < 16:
                    s5_A(u + 2)
                pump()
                if u + 1 < 16:
                    s5_mults(u + 1)
                t, hh2, b, pz4, rz, p0, js = s5_view(u)
                cs = cosT[:, p0:p0 + 2, :]
                sn = sinT[:, p0:p0 + 2, :]
                r_x, r_w, r_p = r_xh2[b], r_ww2[b], r_pr2[b]
                for j in range(2):
                    pair = p0 + j
                    for ri in range(2):
                        V(lambda e, j=j, pair=pair, ri=ri: e.tensor_tensor_scan(out=ww[:, ri, 2 * b + j, :], data0=sp_[:, 2, pair:pair + 1].to_broadcast([128, L]), data1=xh[:, ri, 2 * b + j, :], initial=winit[:, ri, pair:pair + 1], op0=ALU.mult, op1=ALU.add), [r_sp, r_x[ri], r_winit], [r_w[2 * ri + j]])
                G(lambda e: e.tensor_copy(out=zl[:, :, p0:p0 + 2], in_=ww[:, :, js, L - 1]), r_w, [r_zl])
                flush()
                RP = r_w + [r_cosT, r_sinT]
                G(lambda e: e.tensor_tensor(out=pr[:, 0:2, js, :], in0=ww[:, 0:2, js, :], in1=bc3(cs, [128, 2, 2, 128], 1), op=ALU.mult), RP, [r_p])
                G(lambda e: e.tensor_tensor(out=pr[:, 2:4, js, :], in0=ww[:, 0:2, js, :], in1=bc3(sn, [128, 2, 2, 128], 1), op=ALU.mult), RP, [r_p])
                py, ry = pm[t % 2]
                n = 0
                for j in range(2):
                    pair = p0 + j
                    for var, wv in ((0, 0), (1, 2), (2, 2), (3, 1)):
                        T(lambda e, pair=pair, var=var, wv=wv, j=j, n=n: e.matmul(py[64 * hh2:64 * hh2 + 64, :L], lhsT=Cw[:, pair, wv, :], rhs=pr[:, var, 2 * b + j, :], start=(n == 0), stop=False), [r_Cw, r_p], [ry])
                        n += 1
                dgt, r_dgt = dgts[t % 2]
                if hh2 == 0:
                    A(lambda e: e.activation(out=dgt[:, :], in_=identb[:, :], func=AF.Identity, scale=s5d[:, t:t + 1]), [r_identb, r_s5d], [r_dgt])
                T(lambda e: e.matmul(py[64 * hh2:64 * hh2 + 64, :L], lhsT=dgt[:, 64 * hh2:64 * hh2 + 64], rhs=ub[:, t, :L], start=False, stop=True), [r_dgt, r_ub], [ry])
                if hh2 == 1:
                    def epi():
                        A(lambda e: e.activation(out=gel[:, t, :L], in_=py[:, :L], func=AF.Gelu), [ry], [r_gel])
                    if u == 15:
                        flush()
                        epi()
                    else:
                        deferred_next.append((None, epi))
                if u == 15:
                    stage('s5', [(gel[:].rearrange("p a b -> p (a b)"), r_gel, 1024), (gel[:].rearrange("p a b -> p (a b)"), r_gel, 1024), (zl[:].rearrange("p a b -> p (a b)"), r_zl, 64)], ci)
                    RC = [r_zl, r_sp, r_winit, r_car]
                    tA = car[:, 0:32]
                    tB = car[:, 32:64]
                    V(lambda e: e.tensor_tensor(out=tA, in0=zl[:, 0, :], in1=Ctab, op=ALU.mult), RC, [r_car])
                    V(lambda e: e.tensor_tensor(out=tB, in0=zl[:, 1, :], in1=Stab, op=ALU.mult), RC, [r_car])
                    V(lambda e: e.tensor_tensor(out=winit[:, 0, :], in0=tA, in1=tB, op=ALU.subtract), RC, [r_winit])
                    V(lambda e: e.tensor_tensor(out=tA, in0=zl[:, 1, :], in1=Ctab, op=ALU.mult), RC, [r_car])
                    V(lambda e: e.tensor_tensor(out=tB, in0=zl[:, 0, :], in1=Stab, op=ALU.mult), RC, [r_car])
                    V(lambda e: e.tensor_tensor(out=winit[:, 1, :], in0=tA, in1=tB, op=ALU.add), RC, [r_winit])
            if ci + 1 < len(chunks):
                p1.append(lambda: q3.extend(make_inproj(ci + 1)))
            for u in range(16):
                p1.append(lambda u=u: s5_unit(u))

            if not real:
                return p1, p3

            def pc(name, M, E=None, barrier=False, needs=None):
                p3.append({'key': (ci, name), 'M': M, 'E': E, 'barrier': barrier, 'needs': (ci, needs) if needs is not None else None})

            pc('nssd', lambda: norm_T(yg[:, :], [r_yg], g_ssd, 128, 1, mixT, 0))

            def glu_M(jh):
                banks = [pq3[0], pq3[1]] if jh == 0 else [pq3[2], pq3[0]]
                for idx, blk in enumerate((jh, jh + 2)):
                    wt, rw = load_w(OFF_GLU + blk)
                    pp, rp = banks[idx]
                    for kc in range(8):
                        T(lambda e, kc=kc, pp=pp, wt=wt: e.matmul(pp[:, :], lhsT=gel[:, kc, :], rhs=wt[:, kc, :], start=(kc == 0), stop=False), [rw, r_gel], [rp])
                    T(lambda e, pp=pp, blk=blk: e.matmul(pp[:, :], lhsT=onesb[0:1, :], rhs=bglu[0:1, 512 * blk:512 * (blk + 1)], start=False, stop=True), [r_onesb, r_bglu], [rp])

            def glu_E(jh):
                banks = [pq3[0], pq3[1]] if jh == 0 else [pq3[2], pq3[0]]
                (p1_, r1_), (p2_, r2_) = banks
                sg, r_sg = raccs[jh]
                A(lambda e: e.activation(out=sg[:, :], in_=p2_[:, :], func=AF.Sigmoid), [r2_], [r_sg])
                V(lambda e: e.tensor_tensor(out=hh[:, 512 * jh:512 * (jh + 1)], in0=p1_[:, :], in1=sg[:, :], op=ALU.mult), [r1_, r_sg], [r_hh])
            pc('gluA', lambda: glu_M(0), lambda: glu_E(0), barrier=True)
            pc('gluB', lambda: glu_M(1), lambda: glu_E(1), needs='gluA')

            def glu_post():
                norm_T(hh[:, :], [r_hh], g_s5, 128, 2, mixT, 8)
            pc('glupost', glu_post, barrier=True)

            def out_M(ch, kh):
                pp, rp = pq3[ch]
                if ch == 0 and kh == 0:
                    LD(lambda e: e.dma_start(out=hh[:, :], in_=x_d[t0:t0 + L, :]), w=[r_hh])
                wt, rw = load_w(OFF_OUT + 2 * kh + ch)
                for kc in range(8):
                    T(lambda e, kc=kc: e.matmul(pp[:, :], lhsT=mixT[:, 8 * kh + kc, :], rhs=wt[:, kc, :], start=(kh == 0 and kc == 0), stop=False), [rw, r_mixT], [rp])
                if kh == 1:
                    T(lambda e: e.matmul(pp[:, :], lhsT=ident, rhs=hh[:, 512 * ch:512 * (ch + 1)], start=False, stop=True), [r_consts, r_hh], [rp])

            def out_E(ch):
                pp, rp = pq3[ch]
                A(lambda e: e.activation(out=hh[:, 512 * ch:512 * (ch + 1)], in_=pp[:, :], func=AF.Identity), [rp], [r_hh])
            for ch in range(2):
                for kh in range(2):
                    pc('out%d%d' % (ch, kh), lambda ch=ch, kh=kh: out_M(ch, kh), (lambda ch=ch: out_E(ch)) if kh == 1 else None)

            def mlp_norm():
                stage('mix', [(hh[:, :], r_hh, 1024), (mixT[:].rearrange("p a b -> p (a b)"), r_mixT, 2048)], ci)
                norm_T(hh[:, :], [r_hh], g_mlp, 128, 3, hnT, 0)
            pc('mlpnorm', mlp_norm, barrier=True)

            wts = {}

            def up_M(blk):
                wt, rw = load_w(OFF_UP + blk)
                pp, rp = pq3[blk % 3]
                for m in range(4):
                    for kc in range(8):
                        T(lambda e, m=m, kc=kc: e.matmul(pp[:, 128 * m:128 * (m + 1)], lhsT=wt[:, kc, 128 * m:128 * (m + 1)], rhs=hnT[:, kc, :], start=(kc == 0), stop=(kc == 7)), [rw, r_hnT], [rp])

            def up_E(blk):
                pp, rp = pq3[blk % 3]
                rc, r_rc = raccs[blk % 2]
                A(lambda e: e.activation(out=rc[:, :], in_=pp[:, :], func=AF.Relu), [rp], [r_rc])
                G(lambda e: e.tensor_tensor(out=aT[:, 4 * blk:4 * blk + 4, :], in0=rc[:, :].rearrange("p (a b) -> p a b", a=4), in1=rc[:, :].rearrange("p (a b) -> p a b", a=4), op=ALU.mult), [r_rc], [r_aT])
            for blk in range(8):
                pc('up%d' % blk, lambda blk=blk: up_M(blk), lambda blk=blk: up_E(blk), needs=('up%d' % (blk - 3) if blk >= 3 else None))

            def dn_M(ch, kq):
                pp, rp = pq3[ch]
                wt, rw = load_w(OFF_DN + 2 * kq + ch)
                for kc in range(8):
                    T(lambda e, kc=kc: e.matmul(pp[:, :], lhsT=aT[:, 8 * kq + kc, :], rhs=wt[:, kc, :], start=(kq == 0 and kc == 0), stop=False), [rw, r_aT], [rp])
                if kq == 3:
                    T(lambda e: e.matmul(pp[:, :], lhsT=ident, rhs=hh[:, 512 * ch:512 * (ch + 1)], start=False, stop=True), [r_consts, r_hh], [rp])

            def dn_E(ch):
                pp, rp = pq3[ch]
                A(lambda e: e.activation(out=hh[:, 512 * ch:512 * (ch + 1)], in_=pp[:, :], func=AF.Identity), [rp], [r_hh])
            for ch in range(2):
                for kq in range(4):
                    pc('dn%d%d' % (ch, kq), lambda ch=ch, kq=kq: dn_M(ch, kq), (lambda ch=ch: dn_E(ch)) if kq == 3 else None, barrier=(ch == 0 and kq == 0))

            def final():
                rstd_of(hh[:, :], 128, 4, [r_hh])
                V(lambda e: e.scalar_tensor_tensor(out=yg[:, :], in0=hh[:, :], scalar=st[:, 4:5], in1=g_fin, op0=ALU.mult, op1=ALU.mult), [r_hh, r_st, r_vecs], [r_yg])
                GD(lambda e: e.dma_start(out=out_d[128 * (ci - 1):128 * ci, :], in_=yg[:, :]), [r_yg], [r_out])
            pc('final', final, barrier=True)
            return p1, p3

        try:
            for ci, (t0, _L) in enumerate(chunks):
                if stopped[0]:
                    break
                guard = 0
                stale = lambda k: k is not None and (k[0] <= ci - 2 or (k[0] == ci and k[1].startswith('ip')))
                while guard < 1000 and (any(stale(p['key']) for p in q3) or any(stale(k) for k, _ in deferred + deferred_next + eq)):
                    slots_left[0] = 1
                    pump()
                    flush()
                    guard += 1
                p1, p3 = make_chunk(ci, t0)
                slots_left[0] = 20
                for f in p1:
                    f()
                if stop is not None:
                    for pcd in p3:
                        pcd['M']()
                        if pcd['E'] is not None:
                            pcd['E']()
                else:
                    q3.extend(p3)
            guard = 0
            while (q3 or deferred or deferred_next or eq) and guard < 1000:
                slots_left[0] = 1
                pump()
                flush()
                guard += 1
        except _Stop:
            pass
        P.wait_all('pool', [r_out, r_dbg])
        P.emit()
    return nc


def _prep_inputs(inp, b, nchunks=NCH):
    f = lambda a: np.ascontiguousarray(np.asarray(a, dtype=np.float32))
    x = f(inp['x'])[b]
    meta = f(inp['meta_tokens'])
    xs = np.concatenate([np.zeros((112, 1024), np.float32), meta, x[:128 * nchunks]], axis=0)
    w_in = f(inp['w_in'])[0]
    o_xbc, o_dt, o_u = 1024, 1024 + 1536, 1024 + 1536 + 16
    w_in_p = np.zeros((1024, 4096), np.float32)
    w_in_p[:, 0:1536] = w_in[:, o_xbc:o_dt]
    w_in_p[:, 1536:2560] = w_in[:, o_u:]
    w_in_p[:, 2560:3584] = w_in[:, 0:1024]
    w_in_p[:, 3584:3600] = w_in[:, o_dt:o_u]
    gfm = np.concatenate([f(inp[k])[0].reshape(8, 128).T for k in ('g_mix', 'g_ssd', 'g_s5', 'g_mlp')], axis=1)
    vecs = np.concatenate([f(inp['g_final']), f(inp['dt_bias'])[0], f(inp['a_log'])[0], f(inp['d_ssd'])[0]])[None, :].repeat(128, axis=0)
    bglu = f(inp['b_glu'])[0][None, :].repeat(128, axis=0)
    cw = f(inp['conv_w'])[0]
    cb = f(inp['conv_b'])[0]
    cp = np.concatenate([cw, cb[None, :]], axis=0)
    convp = cp.reshape(5, 12, 128).transpose(2, 1, 0).reshape(128, 60)
    ii = np.arange(128)
    ident = np.eye(128, dtype=np.float32)
    tri = (ii[:, None] <= ii[None, :]).astype(np.float32)
    maskneg = np.where(ii[None, :] < ii[:, None], -30000.0, 0.0).astype(np.float32)
    jtab = np.broadcast_to((ii - 127).astype(np.float32)[None, :], (128, 128))
    m112 = np.zeros((128, 128), np.float32); m112[112:, :] = 1.0
    consts = np.concatenate([ident, tri, maskneg, jtab, m112], axis=1)
    gp = lambda a: f(a).reshape(32, 2, 64).transpose(1, 2, 0).reshape(128, 32)
    ls = np.repeat(f(inp['log_step'])[0][:, None], 64, axis=1)
    s5p = np.concatenate([gp(f(inp['lam_re'])[0]), gp(f(inp['lam_im'])[0]), gp(ls)], axis=1)
    gph = lambda a: a.reshape(32, 2, 64, 16).transpose(1, 2, 0, 3).reshape(128, 32, 16)
    bre = gph(f(inp['b_re'])[0]); bim = gph(f(inp['b_im'])[0])
    s5b = np.stack([bre, bim], axis=2).reshape(128, 32 * 2 * 16)
    cre = gph(f(inp['c_re'])[0].transpose(0, 2, 1)); cim = gph(f(inp['c_im'])[0].transpose(0, 2, 1))
    s5c = np.stack([cre, cim], axis=2).reshape(128, 32 * 2 * 16)
    s5d = f(inp['d_s5'])[0].reshape(8, 128).T
    m = {
        'x': xs, 'w_in': w_in_p, 'w_glu': f(inp['w_glu'])[0], 'w_out': f(inp['w_out'])[0],
        'w_up': f(inp['w_up'])[0], 'w_down': f(inp['w_down'])[0], 'vecs': vecs, 'bglu': bglu, 'convp': convp,
        'consts': consts, 'gfm': gfm, 's5p': s5p, 's5b': s5b, 's5c': s5c, 's5d': s5d,
    }
    return {k: np.ascontiguousarray(v, dtype=np.float32) for k, v in m.items()}


def kernel(**inputs):
    nc = build(NCH)
    maps = [_prep_inputs(inputs, 0), _prep_inputs(inputs, 1)]
    in_maps = [maps[c % 2] for c in range(8)]
    res = run_bass_kernel_spmd(nc, in_maps, core_ids=list(range(8)))
    out = np.stack([res.results[0]['out'], res.results[1]['out']], axis=0)
    return out.astype(np.float32)
```
